# Optimizing a Trainium2 kernel written in Bass

```python
import math
import jax, jax.numpy as jnp
from jax import lax
import numpy as np

D_MODEL = 1024
BATCH = 8
SEQ = 4096
DEPTH = 2

GRID_W = 64
Q_BLOCK = 128
EPS = 1e-6
ROPE_THETA = 500000.0
AXIAL_THETA = 10000.0

MLA_HEADS = 8
MLA_Q_RANK = 192
MLA_KV_RANK = 128
MLA_NOPE_DIM = 64
MLA_ROPE_DIM = 32
MLA_V_DIM = 64

DIFF_HEADS = 4
DIFF_DIM = 64
DIFF_V_DIM = 2 * DIFF_DIM
DIFF_ROT = DIFF_DIM // 4

GQA_HEADS = 8
GQA_KV_HEADS = 2
GQA_GROUP = GQA_HEADS // GQA_KV_HEADS
GQA_DIM = 128

FFN_HIDDEN = -(-8 * D_MODEL // (3 * 256)) * 256

EVEN_IN = MLA_Q_RANK + MLA_KV_RANK + MLA_ROPE_DIM + 3 * DIFF_HEADS * DIFF_V_DIM
EVEN_MIX = MLA_HEADS * MLA_V_DIM + DIFF_HEADS * DIFF_V_DIM
ODD_IN = (GQA_HEADS + 2 * GQA_KV_HEADS) * GQA_DIM
ODD_MIX = GQA_HEADS * GQA_DIM
N_EVEN = (DEPTH + 1) // 2
N_ODD = DEPTH // 2

kernel_name = "hybrid_mla_diff_gqa_axial_encoder"


def rmsnorm(x, g):
    xf = x.astype(jnp.float32)
    y = xf * lax.rsqrt(jnp.mean(xf * xf, axis=-1, keepdims=True) + EPS)
    return y.astype(x.dtype) * g


def rope_cos_sin(pos, dim, theta):
    inv = theta ** (-jnp.arange(0, dim, 2, dtype=jnp.float32) / dim)
    ang = pos.astype(jnp.float32)[:, None] * inv[None, :]
    return jnp.cos(ang), jnp.sin(ang)


def apply_rope(x, cos, sin):
    half = x.shape[-1] // 2
    xf = x.astype(jnp.float32)
    x1, x2 = xf[..., :half], xf[..., half:]
    bshape = (cos.shape[0],) + (1,) * (x.ndim - 3) + (half,)
    c, s = cos.reshape(bshape), sin.reshape(bshape)
    return jnp.concatenate([x1 * c - x2 * s, x2 * c + x1 * s], axis=-1).astype(x.dtype)


def partial_rope(x, cos, sin, rot):
    return jnp.concatenate([apply_rope(x[..., :rot], cos, sin), x[..., rot:]], axis=-1)


def axial_rope(x, row, col):
    half = x.shape[-1] // 2
    cr, sr = rope_cos_sin(row, half, AXIAL_THETA)
    cc, sc = rope_cos_sin(col, half, AXIAL_THETA)
    return jnp.concatenate([apply_rope(x[..., :half], cr, sr),
                            apply_rope(x[..., half:], cc, sc)], axis=-1)


def to_blocks(t):
    b, s = t.shape[:2]
    return jnp.moveaxis(t.reshape(b, s // Q_BLOCK, Q_BLOCK, *t.shape[2:]), 1, 0)


def from_blocks(t):
    nb, b, qb = t.shape[:3]
    return jnp.moveaxis(t, 0, 1).reshape(b, nb * qb, *t.shape[3:])


def grouped_attention(q, k, v):
    scale = q.shape[-1] ** -0.5

    def one_block(qb):
        s = jnp.einsum('bqhgd,bkhd->bhgqk', qb, k).astype(jnp.float32) * scale
        p = jax.nn.softmax(s, axis=-1).astype(v.dtype)
        return jnp.einsum('bhgqk,bkhd->bqhgd', p, v)

    return from_blocks(lax.map(one_block, to_blocks(q)))


def diff_attention(q, k, v, lam):
    scale = q.shape[-1] ** -0.5
    lam = lam.astype(jnp.float32)

    def one_block(qb):
        s = jnp.einsum('bqhcd,bkhcd->bhcqk', qb, k).astype(jnp.float32) * scale
        p = jax.nn.softmax(s, axis=-1)
        w = (p[:, :, 0] - lam * p[:, :, 1]).astype(v.dtype)
        return jnp.einsum('bhqk,bkhe->bqhe', w, v)

    return from_blocks(lax.map(one_block, to_blocks(q)))


def even_mixer(h, pos, layer_idx, w_in, q_norm, w_uq, kv_norm, w_ukv,
               lq1, lk1, lq2, lk2, subln, w_out):
    b, s, _ = h.shape
    proj = h @ w_in
    c_q, c_kv, k_rope, qkv_d = jnp.split(
        proj, [MLA_Q_RANK, MLA_Q_RANK + MLA_KV_RANK,
               MLA_Q_RANK + MLA_KV_RANK + MLA_ROPE_DIM], axis=-1)

    q = (rmsnorm(c_q, q_norm) @ w_uq).reshape(b, s, MLA_HEADS, MLA_NOPE_DIM + MLA_ROPE_DIM)
    kv = (rmsnorm(c_kv, kv_norm) @ w_ukv).reshape(b, s, MLA_HEADS, MLA_NOPE_DIM + MLA_V_DIM)
    cos, sin = rope_cos_sin(pos, MLA_ROPE_DIM, ROPE_THETA)
    q = jnp.concatenate([q[..., :MLA_NOPE_DIM], apply_rope(q[..., MLA_NOPE_DIM:], cos, sin)], axis=-1)
    k_r = apply_rope(k_rope, cos, sin)
    k = jnp.concatenate([kv[..., :MLA_NOPE_DIM],
                         jnp.broadcast_to(k_r[:, :, None, :], (b, s, MLA_HEADS, MLA_ROPE_DIM))], axis=-1)
    v = kv[..., MLA_NOPE_DIM:]
    o_mla = grouped_attention(q[:, :, :, None, :], k, v).reshape(b, s, MLA_HEADS * MLA_V_DIM)

    qd, kd, vd = jnp.split(qkv_d, 3, axis=-1)
    qd = qd.reshape(b, s, DIFF_HEADS, 2, DIFF_DIM)
    kd = kd.reshape(b, s, DIFF_HEADS, 2, DIFF_DIM)
    vd = vd.reshape(b, s, DIFF_HEADS, DIFF_V_DIM)
    cp, sp = rope_cos_sin(pos, DIFF_ROT, ROPE_THETA)
    qd = partial_rope(qd, cp, sp, DIFF_ROT)
    kd = partial_rope(kd, cp, sp, DIFF_ROT)
    lam_init = 0.8 - 0.6 * math.exp(-0.3 * layer_idx)
    lam = (jnp.exp(jnp.sum(lq1.astype(jnp.float32) * lk1.astype(jnp.float32)))
           - jnp.exp(jnp.sum(lq2.astype(jnp.float32) * lk2.astype(jnp.float32))) + lam_init)
    o_d = diff_attention(qd, kd, vd, lam)
    o_d = (rmsnorm(o_d, subln) * (1.0 - lam_init)).reshape(b, s, DIFF_HEADS * DIFF_V_DIM)

    return jnp.concatenate([o_mla, o_d], axis=-1) @ w_out


def odd_mixer(h, row, col, w_qkv, q_norm, k_norm, w_out):
    b, s, _ = h.shape
    proj = h @ w_qkv
    q, k, v = jnp.split(proj, [GQA_HEADS * GQA_DIM, (GQA_HEADS + GQA_KV_HEADS) * GQA_DIM], axis=-1)
    q = rmsnorm(q.reshape(b, s, GQA_HEADS, GQA_DIM), q_norm)
    k = rmsnorm(k.reshape(b, s, GQA_KV_HEADS, GQA_DIM), k_norm)
    v = v.reshape(b, s, GQA_KV_HEADS, GQA_DIM)
    q = axial_rope(q, row, col)
    k = axial_rope(k, row, col)
    o = grouped_attention(q.reshape(b, s, GQA_KV_HEADS, GQA_GROUP, GQA_DIM), k, v)
    return o.reshape(b, s, ODD_MIX) @ w_out


def swiglu(h, w_gate, w_up, w_down):
    return (jax.nn.silu(h @ w_gate) * (h @ w_up)) @ w_down


def setup_inputs(seed: int = 0) -> dict:
    key = jax.random.key(seed)
    ks = jax.random.split(key, 23)

    def w(k, shape, fan_in):
        return jax.random.normal(k, shape, jnp.float32) * (fan_in ** -0.5)

    def gain(k, shape):
        return 1.0 + 0.02 * jax.random.normal(k, shape, jnp.float32)

    def small(k, shape):
        return 0.1 * jax.random.normal(k, shape, jnp.float32)

    return {
        "x": jax.random.normal(ks[0], (BATCH, SEQ, D_MODEL), jnp.float32),
        "e_attn_norm": gain(ks[1], (N_EVEN, D_MODEL)),
        "e_w_in": w(ks[2], (N_EVEN, D_MODEL, EVEN_IN), D_MODEL),
        "e_q_norm": gain(ks[3], (N_EVEN, MLA_Q_RANK)),
        "e_w_uq": w(ks[4], (N_EVEN, MLA_Q_RANK, MLA_HEADS * (MLA_NOPE_DIM + MLA_ROPE_DIM)), MLA_Q_RANK),
        "e_kv_norm": gain(ks[5], (N_EVEN, MLA_KV_RANK)),
        "e_w_ukv": w(ks[6], (N_EVEN, MLA_KV_RANK, MLA_HEADS * (MLA_NOPE_DIM + MLA_V_DIM)), MLA_KV_RANK),
        "e_lambda_q1": small(ks[7], (N_EVEN, DIFF_DIM)),
        "e_lambda_k1": small(ks[8], (N_EVEN, DIFF_DIM)),
        "e_lambda_q2": small(ks[9], (N_EVEN, DIFF_DIM)),
        "e_lambda_k2": small(ks[10], (N_EVEN, DIFF_DIM)),
        "e_subln": gain(ks[11], (N_EVEN, DIFF_V_DIM)),
        "e_w_out": w(ks[12], (N_EVEN, EVEN_MIX, D_MODEL), EVEN_MIX),
        "o_attn_norm": gain(ks[13], (N_ODD, D_MODEL)),
        "o_w_qkv": w(ks[14], (N_ODD, D_MODEL, ODD_IN), D_MODEL),
        "o_q_norm": gain(ks[15], (N_ODD, GQA_DIM)),
        "o_k_norm": gain(ks[16], (N_ODD, GQA_DIM)),
        "o_w_out": w(ks[17], (N_ODD, ODD_MIX, D_MODEL), ODD_MIX),
        "ffn_norm": gain(ks[18], (DEPTH, D_MODEL)),
        "w_gate": w(ks[19], (DEPTH, D_MODEL, FFN_HIDDEN), D_MODEL),
        "w_up": w(ks[20], (DEPTH, D_MODEL, FFN_HIDDEN), D_MODEL),
        "w_down": w(ks[21], (DEPTH, FFN_HIDDEN, D_MODEL), FFN_HIDDEN),
        "final_norm": gain(ks[22], (D_MODEL,)),
    }


def reference(x, e_attn_norm, e_w_in, e_q_norm, e_w_uq, e_kv_norm, e_w_ukv,
              e_lambda_q1, e_lambda_k1, e_lambda_q2, e_lambda_k2, e_subln, e_w_out,
              o_attn_norm, o_w_qkv, o_q_norm, o_k_norm, o_w_out,
              ffn_norm, w_gate, w_up, w_down, final_norm):
    s = x.shape[1]
    rows = s // GRID_W
    pos = jnp.arange(s, dtype=jnp.int32)
    row = jnp.repeat(jnp.arange(rows, dtype=jnp.int32), GRID_W)
    col = jnp.tile(jnp.arange(GRID_W, dtype=jnp.int32), rows)

    h = x
    for i in range(DEPTH):
        j = i // 2
        if i % 2 == 0:
            h = h + even_mixer(rmsnorm(h, e_attn_norm[j]), pos, i, e_w_in[j], e_q_norm[j],
                               e_w_uq[j], e_kv_norm[j], e_w_ukv[j], e_lambda_q1[j],
                               e_lambda_k1[j], e_lambda_q2[j], e_lambda_k2[j],
                               e_subln[j], e_w_out[j])
        else:
            h = h + odd_mixer(rmsnorm(h, o_attn_norm[j]), row, col, o_w_qkv[j],
                              o_q_norm[j], o_k_norm[j], o_w_out[j])
        h = h + swiglu(rmsnorm(h, ffn_norm[i]), w_gate[i], w_up[i], w_down[i])
    return rmsnorm(h, final_norm)
```

```python
import math
from contextlib import ExitStack
import numpy as np
import concourse.bass as bass
import concourse.mybir as mybir
from concourse.bass_utils import run_bass_kernel_spmd

F32 = mybir.dt.float32
BF16 = mybir.dt.bfloat16
AF = mybir.ActivationFunctionType
ALU = mybir.AluOpType
AX = mybir.AxisListType

ENGINES = ("tensor", "vector", "scalar", "gpsimd", "sync")

SEQ = 4096
D = 1024
EPS = 1e-6
FF = 2816
NM = FF // 128


class Buf:
    __slots__ = ("name", "last_w", "readers", "sem", "excl")

    def __init__(self, name):
        self.name = name
        self.excl = False
        self.last_w = None
        self.readers = {}
        self.sem = None


class Op:
    __slots__ = ("eng", "fn", "deps", "needed", "sig", "dma_buf")

    def __init__(self, eng, fn):
        self.eng = eng
        self.fn = fn
        self.deps = []
        self.needed = False
        self.sig = None
        self.dma_buf = None


class Sched:
    def __init__(self, nc, stack, n_dma_sems=18, n_sw_sems=16):
        self.nc = nc
        self.esem = {e: stack.enter_context(nc.semaphore("s_" + e)) for e in ENGINES}
        self.ecnt = {e: 0 for e in ENGINES}
        self.pool = [[stack.enter_context(nc.semaphore("d%d" % i)), 0] for i in range(n_dma_sems)]
        self.free = {"sync": list(self.pool), "gpsimd": [[stack.enter_context(nc.semaphore("w%d" % i)), 0] for i in range(n_sw_sems)]}
        self.free["scalar"] = self.free["sync"]
        self.known = {e: {} for e in ENGINES}
        self.reset()

    def reset(self):
        self.ops = {e: [] for e in ENGINES}
        self.all_ops = []
        self.phase_bufs = []

    def op(self, eng, fn, reads=(), writes=(), dma_buf=None, fill=False):
        o = Op(eng, fn)
        o.dma_buf = dma_buf
        writes = list(writes) + [b for b in reads if b.excl]
        reads = [b for b in reads if not b.excl]
        deps = {}
        for b in reads:
            if b.last_w is not None:
                deps[id(b.last_w)] = b.last_w
        for b in writes:
            if b.last_w is not None and not (fill and b.last_w.dma_buf is dma_buf and b.last_w.eng == eng):
                deps[id(b.last_w)] = b.last_w
            for r in b.readers.values():
                deps[id(r)] = r
        key = ("dma", id(dma_buf)) if dma_buf is not None else eng
        for b in reads:
            b.readers[key] = o
        for b in writes:
            b.last_w = o
            b.readers = {}
        o.deps = list(deps.values())
        for d in o.deps:
            d.needed = True
        if dma_buf is not None:
            o.needed = True
            if dma_buf.sem is None:
                dma_buf.sem = self.free[eng].pop()
                self.phase_bufs.append((dma_buf, eng))
        self.ops[eng].append(o)
        self.all_ops.append(o)
        return o

    def barrier(self):
        last = {}
        for o in self.all_ops:
            if o.fn is None:
                continue
            key = ("dma", id(o.dma_buf)) if o.dma_buf is not None else o.eng
            last[key] = o
        lasts = list(last.values())
        for e in ENGINES:
            o = Op(e, None)
            o.deps = list(lasts)
            for d in o.deps:
                d.needed = True
            self.ops[e].append(o)
            self.all_ops.append(o)

    def emit(self):
        self.barrier()
        nc = self.nc
        for o in self.all_ops:
            if o.fn is None:
                continue
            if o.dma_buf is not None:
                s = o.dma_buf.sem
                s[1] += 16
                o.sig = (s[0], s[1], 16)
            elif o.needed:
                self.ecnt[o.eng] += 1
                o.sig = (self.esem[o.eng], self.ecnt[o.eng], 1)
        ops = self.ops
        known_all = self.known

        def run(e):
            def body(eng):
                known = known_all[e]
                for o in ops[e]:
                    need = {}
                    for d in o.deps:
                        if d.sig is None:
                            continue
                        if d.eng == e and e == "tensor" and d.dma_buf is None:
                            continue
                        s, v, _ = d.sig
                        k = id(s)
                        if known.get(k, 0) >= v:
                            continue
                        if k not in need or need[k][1] < v:
                            need[k] = (s, v)
                    for k, (s, v) in need.items():
                        eng.wait_ge(s, v)
                        known[k] = v
                    if o.fn is None:
                        continue
                    ins = o.fn(eng)
                    if o.sig is not None:
                        ins.then_inc(o.sig[0], o.sig[2])
            return body

        with nc.Block() as block:
            block.tensor(run("tensor"))
            block.vector(run("vector"))
            block.scalar(run("scalar"))
            block.gpsimd(run("gpsimd"))
            block.sync(run("sync"))
        for b, q in self.phase_bufs:
            self.free[q].append(b.sem)
            b.sem = None
        self.reset()


class Tl:
    def __init__(self, t, name, nchunks=0):
        self.t = t
        self.b = Buf(name)
        self.cb = [Buf("%s_%d" % (name, i)) for i in range(nchunks)]


class Rot:
    def __init__(self, tiles):
        self.tiles = tiles
        self.i = 0

    def next(self):
        t = self.tiles[self.i % len(self.tiles)]
        self.i += 1
        return t


class KB:
    def __init__(self, nc, stack):
        self.nc = nc
        self.S = Sched(nc, stack)
        self.uid = 0

    def sb(self, st, name, shape, dt, nchunks=0):
        self.uid += 1
        nm = "%s_%d" % (name, self.uid)
        return Tl(st.enter_context(self.nc.sbuf_tensor(nm, shape, dt)), nm, nchunks)

    def ps(self, st, name, shape, dt=F32):
        self.uid += 1
        nm = "%s_%d" % (name, self.uid)
        t = Tl(st.enter_context(self.nc.psum_tensor(nm, shape, dt)), nm)
        t.b.excl = True
        return t

    def mm(self, out, lhsT, rhs, start, stop, reads, writes, tp=None):
        if tp is None:
            self.S.op("tensor", lambda e: e.matmul(out, lhsT=lhsT, rhs=rhs, start=start, stop=stop), reads, writes)
        else:
            self.S.op("tensor", lambda e: e.matmul(out, lhsT=lhsT, rhs=rhs, start=start, stop=stop, tile_position=tp), reads, writes)

    def act(self, out, in_, func, reads, writes, scale=1.0, bias=None):
        if bias is None:
            self.S.op("scalar", lambda e: e.activation(out=out, in_=in_, func=func, scale=scale), reads, writes)
        else:
            self.S.op("scalar", lambda e: e.activation(out=out, in_=in_, func=func, scale=scale, bias=bias), reads, writes)

    def tt(self, eng, out, in0, in1, op, reads, writes):
        self.S.op(eng, lambda e: e.tensor_tensor(out=out, in0=in0, in1=in1, op=op), reads, writes)

    def stt(self, out, in0, scalar, in1, op0, op1, reads, writes):
        self.S.op("vector", lambda e: e.scalar_tensor_tensor(out=out, in0=in0, scalar=scalar, in1=in1, op0=op0, op1=op1), reads, writes)

    def ts(self, eng, out, in0, s1, op0, reads, writes, s2=None, op1=None):
        if op1 is None:
            self.S.op(eng, lambda e: e.tensor_scalar(out=out, in0=in0, scalar1=s1, scalar2=None, op0=op0), reads, writes)
        else:
            self.S.op(eng, lambda e: e.tensor_scalar(out=out, in0=in0, scalar1=s1, scalar2=s2, op0=op0, op1=op1), reads, writes)

    def cp(self, eng, out, in_, reads, writes):
        if eng == "scalar":
            self.S.op(eng, lambda e: e.activation(out=out, in_=in_, func=AF.Copy), reads, writes)
        else:
            self.S.op(eng, lambda e: e.tensor_copy(out=out, in_=in_), reads, writes)

    def recip(self, out, in_, reads, writes):
        self.S.op("vector", lambda e: e.reciprocal(out=out, in_=in_), reads, writes)

    def memset(self, eng, ap, val, writes):
        self.S.op(eng, lambda e: e.memset(ap, val), (), writes)

    def dma(self, q, out, in_, reads, writes, buf, fill=False):
        self.S.op(q, lambda e: e.dma_start(out=out, in_=in_), reads, writes, dma_buf=buf, fill=fill)

    def load_w_thunks(self, stg, dram, W, wbuf, nk, c0, c1, dst_c0=0, rows_last=128, engs=None, CH=2048):
        th = []
        for k in range(nk):
            rows = 128 if k < nk - 1 else rows_last
            for cc in range(c0, c1, CH):
                ce = min(c1, cc + CH)

                def one(k=k, rows=rows, cc=cc, ce=ce):
                    d0 = dst_c0 + cc - c0
                    self.dma("gpsimd", W.t[0:rows, k, d0:d0 + ce - cc], dram[k * 128:k * 128 + rows, cc:ce], (), [wbuf], wbuf, fill=True)
                th.append(one)
        return th

    def load_w(self, *a, **kw):
        for t in self.load_w_thunks(*a, **kw):
            t()


def build_program(stop_after=99, dbg=False, skip=()):
    nc = bass.Bass("TRN2", target_bir_lowering=False)

    def din(name, shape, dt=F32):
        return nc.dram_tensor(name, shape, dt, kind="ExternalInput").ap()

    def dscr(name, shape, dt):
        kind = "ExternalOutput" if dbg else "Internal"
        return nc.dram_tensor(name, shape, dt, kind=kind).ap()

    xT = din("xT", [D, SEQ])
    gains = din("gains", [128, 40])
    svec = din("svec", [128, 8])
    lamv = din("lamv", [128, 4, 64])
    sel_d = din("sel", [128, 128])
    pms_d = din("pms", [128, 256])
    e_w_in = din("e_w_in", [D, 1888])
    e_w_in_p = din("e_w_in_p", [D, 1056])
    e_w_uq = din("e_w_uq", [192, 768])
    e_w_uq_p = din("e_w_uq_p", [192, 768])
    e_w_ukv = din("e_w_ukv", [128, 1024])
    e_w_out = din("e_w_out", [D, D])
    o_w_qkv = din("o_w_qkv", [D, 1536])
    o_w_qkv_p = din("o_w_qkv_p", [D, 1280])
    o_w_out = din("o_w_out", [D, D])
    w_gate = din("w_gate", [2, D, FF])
    w_up = din("w_up", [2, D, FF])
    w_down = din("w_down", [2, FF, D])
    tab_m = din("tab_m", [2, 96, SEQ])
    tab_kr = din("tab_kr", [2, 32, SEQ])
    tab_d = din("tab_d", [2, 128, SEQ])
    tab_g = din("tab_g", [2, 128, SEQ])
    outT = nc.dram_tensor("outT", [D, SEQ], F32, kind="ExternalOutput").ap()

    qm = dscr("qm", [8, 96, SEQ], BF16)
    km = dscr("km", [8, 96, SEQ], BF16)
    vm = dscr("vm", [8, 128, SEQ], BF16)
    qd = dscr("qd", [4, 128, SEQ], BF16)
    kd = dscr("kd", [4, 128, SEQ], BF16)
    vd = dscr("vd", [4, 128, SEQ], BF16)
    aT = dscr("aT", [D, SEQ], BF16)
    h1T = dscr("h1T", [D, SEQ], F32)
    h2T = dscr("h2T", [D, SEQ], F32)
    h3T = dscr("h3T", [D, SEQ], F32)
    wg16 = nc.dram_tensor("wg16", [D, FF], BF16, kind="Internal").ap()
    wu16 = nc.dram_tensor("wu16", [D, FF], BF16, kind="Internal").ap()
    wd16 = nc.dram_tensor("wd16", [FF, D], BF16, kind="Internal").ap()

    def cv(ap):
        return ap.rearrange("(c p) t -> p c t", p=128)

    with ExitStack() as gst:
        kb = KB(nc, gst)
        S = kb.S
        ones = kb.sb(gst, "ones", [128, 128], BF16)
        gn = kb.sb(gst, "gains", [128, 40], F32)
        sv = kb.sb(gst, "svec", [128, 8], F32)
        lam_t = kb.sb(gst, "lamt", [128, 8], F32)
        kb.memset("vector", ones.t[:], 1.0, [ones.b])
        kb.dma("sync", gn.t[:], gains, (), [gn.b], gn.b)
        kb.dma("sync", sv.t[:], svec, (), [sv.b], sv.b)
        sel = kb.sb(gst, "sel", [128, 128], F32)
        kb.dma("sync", sel.t[:], sel_d, (), [sel.b], sel.b)
        PM = kb.sb(gst, "PM", [128, 256], BF16)
        kb.dma("gpsimd", PM.t[:], pms_d, (), [PM.b], PM.b)

        def norm_block(st, hb, hn, T, gcol, sq_rot, ssq, rs):
            for c in range(8):
                sq = sq_rot.next()
                kb.act(sq.t[:, 0:T], hb.t[:, c, :], AF.Square, [hb.b] + hb.cb[c:c + 1], [sq.b])
                kb.mm(ssq.t[:, 0:T], ones.t[:], sq.t[:, 0:T], c == 0, c == 7, [ones.b, sq.b], [ssq.b])
            kb.act(rs.t[:, 0:T], ssq.t[:, 0:T], AF.Ln, [ssq.b], [rs.b], scale=1.0 / D, bias=EPS)
            kb.act(rs.t[:, 0:T], rs.t[:, 0:T], AF.Exp, [rs.b], [rs.b], scale=-0.5)
            for c in range(8):
                kb.stt(hn.t[:, c, :], hb.t[:, c, :], gn.t[:, gcol + c:gcol + c + 1], rs.t[:, 0:T],
                       ALU.mult, ALU.mult, [hb.b, gn.b, rs.b] + hb.cb[c:c + 1], [hn.b])

        def head_rstd(A, rows, nfeat, sq, ssq, rs, extra=None):
            kb.act(sq.t[0:rows, :], A.t[0:rows, :], AF.Square, [A.b], [sq.b])
            if extra is None:
                kb.mm(ssq.t[:], ones.t[0:rows, :], sq.t[0:rows, :], True, True, [ones.b, sq.b], [ssq.b])
            else:
                A2, rows2, sq2 = extra
                kb.act(sq2.t[0:rows2, :], A2.t[0:rows2, :], AF.Square, [A2.b], [sq2.b])
                kb.mm(ssq.t[:], ones.t[0:rows, :], sq.t[0:rows, :], True, False, [ones.b, sq.b], [ssq.b])
                kb.mm(ssq.t[:], ones.t[0:rows2, :], sq2.t[0:rows2, :], False, True, [ones.b, sq2.b], [ssq.b])
            kb.act(rs.t[:], ssq.t[:], AF.Ln, [ssq.b], [rs.b], scale=1.0 / nfeat, bias=EPS)
            kb.act(rs.t[:], rs.t[:], AF.Exp, [rs.b], [rs.b], scale=-0.5)

        if 1 not in skip:
            with ExitStack() as st:
                T = 512
                Win = kb.sb(st, "Win", [128, 8, 1888], BF16)
                Winp = kb.sb(st, "Winp", [128, 8, 32], BF16)
                abfs = Rot([kb.sb(st, "abf", [128, T], BF16) for _ in range(3)])
                Wuq = kb.sb(st, "Wuq", [128, 2, 768], BF16)
                Wuqp = kb.sb(st, "Wuqp", [128, 2, 768], BF16)
                Wukv = kb.sb(st, "Wukv", [128, 1, 1024], BF16)
                stg = None
                Win_b2 = Buf("Win_b2"); Winp_b2 = Buf("Winp_b2")
                kb.load_w(stg, e_w_in, Win, Win.b, 8, 0, 352)
                kb.load_w(stg, e_w_in_p, Winp, Winp.b, 8, 0, 32)
                kb.load_w(stg, e_w_uq, Wuq, Wuq.b, 2, 0, 768, rows_last=64)
                kb.load_w(stg, e_w_uq_p, Wuqp, Wuqp.b, 2, 0, 768, rows_last=64)
                kb.load_w(stg, e_w_ukv, Wukv, Wukv.b, 1, 0, 1024)
                kb.load_w(stg, e_w_in, Win, Win_b2, 8, 352, 1888, dst_c0=352)
                hbs = Rot([kb.sb(st, "hb", [128, 8, T], F32) for _ in range(2)])
                hns = Rot([kb.sb(st, "hn", [128, 8, T], BF16) for _ in range(2)])
                tms = Rot([kb.sb(st, "tm", [96, 2, T], F32) for _ in range(2)])
                tks = Rot([kb.sb(st, "tk", [32, 2, T], F32) for _ in range(2)])
                tds = Rot([kb.sb(st, "td", [128, 2, T], F32) for _ in range(2)])
                sqs = Rot([kb.sb(st, "sq", [128, T], BF16) for _ in range(3)])
                rss = Rot([kb.sb(st, "rs", [128, T], F32) for _ in range(2)])
                t1s = Rot([kb.sb(st, "t1", [128, T], F32) for _ in range(3)])
                t2s = Rot([kb.sb(st, "t2", [128, T], F32) for _ in range(3)])
                obs = Rot([kb.sb(st, "ob", [128, T], BF16) for _ in range(4)])
                obs2 = Rot([kb.sb(st, "ob2", [128, T], BF16) for _ in range(4)])
                cqn0 = kb.sb(st, "cqn0", [128, T], BF16)
                cqn1 = kb.sb(st, "cqn1", [64, T], BF16)
                ckvn = kb.sb(st, "ckvn", [128, T], BF16)
                vts = Rot([kb.sb(st, "vt", [128, 8, 4, 128], BF16) for _ in range(2)])
                vds = Rot([kb.sb(st, "vdt", [128, 4, 4, 128], BF16) for _ in range(2)])
                vmv = vm.rearrange("h p (j c) -> p h j c", c=128)
                vdv = vd.rearrange("h p (j c) -> p h j c", c=128)
                pps = Rot([kb.ps(st, "pp", [128, 512]) for _ in range(6)])
                ssqs = Rot([kb.ps(st, "ssq", [128, 512]) for _ in range(2)])
                for v in vts.tiles:
                    kb.memset("vector", v.t[:, :, :, 64:128], 1.0, [v.b])

                def rope_out(A, B, rows, tc, tsn, tab_b, dst_dram, eng_add, rot=None):
                    t1 = t1s.next(); t2 = t2s.next(); ob = (rot or obs).next()
                    kb.tt("vector", t1.t[0:rows, :], A.t[0:rows, :], tc, ALU.mult, [A.b, tab_b], [t1.b])
                    kb.tt("vector", t2.t[0:rows, :], B.t[0:rows, :], tsn, ALU.mult, [B.b, tab_b], [t2.b])
                    kb.tt(eng_add, ob.t[0:rows, :], t1.t[0:rows, :], t2.t[0:rows, :], ALU.add, [t1.b, t2.b], [ob.b])
                    return ob

                xTv = cv(xT)
                nblk = SEQ // T
                def load1(tb):
                    hb = hbs.next()
                    kb.dma("sync", hb.t[:], xTv[:, :, tb * T:(tb + 1) * T], (), [hb.b], hb.b)
                    return hb

                hb_next = load1(0)
                hn_next = hns.next()
                norm_block(st, hb_next, hn_next, T, 0, sqs, ssqs.next(), rss.next())
                for tb in range(nblk):
                    t0 = tb * T
                    hn = hn_next
                    tm = tms.next(); tk = tks.next(); td = tds.next()
                    kb.dma("sync", tm.t[:], tab_m[:, :, t0:t0 + T].rearrange("a r t -> r a t"), (), [tm.b], tm.b)
                    kb.dma("sync", tk.t[:], tab_kr[:, :, t0:t0 + T].rearrange("a r t -> r a t"), (), [tk.b], tk.b)
                    kb.dma("sync", td.t[:], tab_d[:, :, t0:t0 + T].rearrange("a r t -> r a t"), (), [td.b], td.b)
                    if tb + 1 < nblk:
                        hb_next = load1(tb + 1)
                    pq0 = pps.next(); pq1 = pps.next(); pkv = pps.next(); pka = pps.next(); pkb = pps.next()
                    for k in range(8):
                        kb.mm(pq0.t[:], Win.t[:, k, 0:128], hn.t[:, k, :], k == 0, k == 7, [Win.b, hn.b], [pq0.b])
                    for k in range(8):
                        kb.mm(pq1.t[0:64, :], Win.t[:, k, 128:192], hn.t[:, k, :], k == 0, k == 7, [Win.b, hn.b], [pq1.b])
                    for k in range(8):
                        kb.mm(pkv.t[:], Win.t[:, k, 192:320], hn.t[:, k, :], k == 0, k == 7, [Win.b, hn.b], [pkv.b])
                    for k in range(8):
                        kb.mm(pka.t[0:32, :], Win.t[:, k, 320:352], hn.t[:, k, :], k == 0, k == 7, [Win.b, hn.b], [pka.b])
                    for k in range(8):
                        kb.mm(pkb.t[0:32, :], Winp.t[:, k, 0:32], hn.t[:, k, :], k == 0, k == 7, [Winp.b, hn.b], [pkb.b])
                    rs = rss.next()
                    head_rstd(pq0, 128, 192, sqs.next(), ssqs.next(), rs, extra=(pq1, 64, sqs.next()))
                    kb.stt(cqn0.t[:], pq0.t[:], sv.t[:, 0:1], rs.t[:], ALU.mult, ALU.mult, [pq0.b, sv.b, rs.b], [cqn0.b])
                    kb.stt(cqn1.t[:], pq1.t[0:64, :], sv.t[0:64, 1:2], rs.t[0:64, :], ALU.mult, ALU.mult, [pq1.b, sv.b, rs.b], [cqn1.b])
                    rs2 = rss.next()
                    head_rstd(pkv, 128, 128, sqs.next(), ssqs.next(), rs2)
                    kb.stt(ckvn.t[:], pkv.t[:], sv.t[:, 2:3], rs2.t[:], ALU.mult, ALU.mult, [pkv.b, sv.b, rs2.b], [ckvn.b])
                    ob = rope_out(pka, pkb, 32, tk.t[:, 0, :], tk.t[:, 1, :], tk.b, None, "gpsimd")
                    for h in range(8):
                        kb.dma("gpsimd", km[h, 64:96, t0:t0 + T], ob.t[0:32, :], [ob.b], (), ob.b)
                    for h in range(8):
                        A = pps.next(); B = pps.next()
                        kb.mm(A.t[0:96, :], Wuq.t[:, 0, h * 96:(h + 1) * 96], cqn0.t[:], True, False, [Wuq.b, cqn0.b], [A.b])
                        kb.mm(A.t[0:96, :], Wuq.t[0:64, 1, h * 96:(h + 1) * 96], cqn1.t[:], False, True, [Wuq.b, cqn1.b], [A.b])
                        kb.mm(B.t[0:96, :], Wuqp.t[:, 0, h * 96:(h + 1) * 96], cqn0.t[:], True, False, [Wuqp.b, cqn0.b], [B.b])
                        kb.mm(B.t[0:96, :], Wuqp.t[0:64, 1, h * 96:(h + 1) * 96], cqn1.t[:], False, True, [Wuqp.b, cqn1.b], [B.b])
                        ob = rope_out(A, B, 96, tm.t[:, 0, :], tm.t[:, 1, :], tm.b, None, "gpsimd" if h % 2 else "vector", rot=obs2)
                        kb.dma("sync", qm[h, :, t0:t0 + T], ob.t[0:96, :], [ob.b], (), ob.b)
                    for h in range(8):
                        A = pps.next(); ob = obs2.next()
                        kb.mm(A.t[0:64, :], Wukv.t[:, 0, h * 128:h * 128 + 64], ckvn.t[:], True, True, [Wukv.b, ckvn.b], [A.b])
                        kb.cp("scalar" if h % 2 else "vector", ob.t[0:64, :], A.t[0:64, :], [A.b], [ob.b])
                        kb.dma("sync", km[h, 0:64, t0:t0 + T], ob.t[0:64, :], [ob.b], (), ob.b)
                    wv = Wukv.t[:, 0, :].rearrange("p (h c) -> p h c", c=128)[:, :, 64:128]
                    vt = vts.next()
                    for tt in range(4):
                        A = pps.next()
                        kb.mm(A.t[:], ckvn.t[:, tt * 128:(tt + 1) * 128], wv, True, True, [Wukv.b, ckvn.b], [A.b])
                        kb.cp("scalar" if tt % 2 else "vector", vt.t[:, :, tt, 0:64], A.t[:].rearrange("p (h c) -> p h c", c=64), [A.b], [vt.b])
                    kb.dma("gpsimd", vmv[:, :, tb * 4:tb * 4 + 4, :], vt.t[:], [vt.b], (), vt.b)
                    if tb + 1 < nblk:
                        hn_next = hns.next()
                        norm_block(st, hb_next, hn_next, T, 0, sqs, ssqs.next(), rss.next())
                    for which, colA, colB, dst in ((0, 352, 32, qd), (1, 864, 544, kd)):
                        for h in range(4):
                            A = pps.next(); B = pps.next()
                            for k in range(8):
                                kb.mm(A.t[:], Win.t[:, k, colA + h * 128:colA + (h + 1) * 128], hn.t[:, k, :], k == 0, k == 7, [Win_b2, hn.b], [A.b])
                            abf = abfs.next()
                            kb.cp("scalar", abf.t[:], A.t[:], [A.b], [abf.b])
                            kb.mm(B.t[:], PM.t[:, 0:128], abf.t[:], True, True, [PM.b, abf.b], [B.b])
                            ob = rope_out(A, B, 128, td.t[:, 0, :], td.t[:, 1, :], td.b, None, "gpsimd")
                            kb.dma("gpsimd", dst[h, :, t0:t0 + T], ob.t[:], [ob.b], (), ob.b)
                    vdt = vds.next()
                    for tt in range(4):
                        A = pps.next()
                        for k in range(8):
                            kb.mm(A.t[:], hn.t[:, k, tt * 128:(tt + 1) * 128], Win.t[:, k, 1376:1888], k == 0, k == 7, [Win_b2, hn.b], [A.b])
                        kb.cp("scalar" if tt % 2 else "vector", vdt.t[:, :, tt, :], A.t[:].rearrange("p (h c) -> p h c", c=128), [A.b], [vdt.b])
                    kb.dma("gpsimd", vdv[:, :, tb * 4:tb * 4 + 4, :], vdt.t[:], [vdt.b], (), vdt.b)
                S.emit()
        if stop_after <= 1:
            return nc

        class Job:
            pass

        def run_jobs(jobs, hooks):
            if not jobs:
                return
            jobs[0].qk(0)
            jobs[0].qk(1)
            for ji, job in enumerate(jobs):
                nxt = jobs[ji + 1] if ji + 1 < len(jobs) else None
                if job.on_start is not None:
                    job.on_start()
                for pk in range(job.n):
                    if pk + 2 < job.n:
                        job.qk(pk + 2)
                    elif nxt is not None:
                        nxt.qk(pk + 2 - job.n)
                    job.pv(pk)
                    for fn in hooks.pop(pk, []):
                        fn()
                job.epilogue()

        def core_job(Kt_ap_fn, Q_ap, Vt_fn, V_b, KQ_reads, scale, Ss, Ps, O, L, epilogue, on_start=None):
            job = Job()
            job.n = 16
            job.on_start = on_start
            job.epilogue = epilogue
            Pq = {}

            def qk(kp):
                Sx = Ss.next()
                for j in range(2):
                    kc = kp * 2 + j
                    kb.mm(Sx.t[:, j * 512:(j + 1) * 512], Kt_ap_fn(kc), Q_ap, True, True, KQ_reads, [Sx.b])
                P = Ps.next()
                kb.act(P.t[:], Sx.t[:], AF.Exp, [Sx.b], [P.b], scale=scale)
                Pq[kp] = P

            def pv(pk):
                PP = Pq[pk]
                for j in range(2):
                    kc = pk * 2 + j
                    kb.mm(O.t[:], Vt_fn(kc), PP.t[:, j * 512:(j + 1) * 512], kc == 0, kc == 31, [V_b, PP.b], [O.b])
                if L is not None and pk % 2 == 1:
                    prevP = Pq[pk - 1]
                    for j, (Pt, hf) in enumerate(((prevP, 0), (prevP, 1), (PP, 0), (PP, 1))):
                        kb.mm(L.t[32 * j:32 * j + 32, :], ones.t[:, 0:32], Pt.t[:, hf * 512:(hf + 1) * 512], pk == 1, pk == job.n - 1,
                              [ones.b, Pt.b], [L.b], tp=(0, 32 * j))
            job.qk = qk
            job.pv = pv
            return job

        def combine_L(L, lc, p0, p1):
            kb.cp("vector", lc.t[p0:p1, :], L.t[p0:p1, :], [L.b], [lc.b])
            kb.mm(L.t[:], sel.t[p0:p1, :], lc.t[p0:p1, :], True, True, [sel.b, lc.b], [L.b])

        def diff_job(Kt, Qt, q0, Vt, scale, Ss, Ps, O0, O1, L01, epilogue, on_start=None):
            job = Job()
            job.n = 32
            job.on_start = on_start
            job.epilogue = epilogue
            Pq = {}

            def qk(kp):
                Sx = Ss.next()
                for c in range(2):
                    kb.mm(Sx.t[:, c * 512:(c + 1) * 512], Kt.t[64 * c:64 * c + 64, kp * 128:(kp + 1) * 128],
                          Qt.t[64 * c:64 * c + 64, q0:q0 + 512], True, True, [Kt.b, Qt.b], [Sx.b])
                P = Ps.next()
                kb.act(P.t[:], Sx.t[:], AF.Exp, [Sx.b], [P.b], scale=scale)
                Pq[kp] = P

            def pv(pk):
                PP = Pq[pk]
                kb.mm(O0.t[:], Vt.t[:, pk, :], PP.t[:, 0:512], pk == 0, pk == job.n - 1, [Vt.b, PP.b], [O0.b])
                kb.mm(O1.t[:], Vt.t[:, pk, :], PP.t[:, 512:1024], pk == 0, pk == job.n - 1, [Vt.b, PP.b], [O1.b])
                if pk % 2 == 1:
                    prevP = Pq[pk - 1]
                    for j, (Pt, c) in enumerate(((prevP, 0), (PP, 0), (prevP, 1), (PP, 1))):
                        kb.mm(L01.t[32 * j:32 * j + 32, :], ones.t[:, 0:32], Pt.t[:, c * 512:(c + 1) * 512], pk == 1, pk == job.n - 1,
                              [ones.b, Pt.b], [L01.b], tp=(0, 32 * j))
            job.qk = qk
            job.pv = pv
            return job

        stF = gst.enter_context(ExitStack())
        Wg0 = kb.sb(stF, "Wg0", [128, 8, FF], BF16)
        Wu0 = kb.sb(stF, "Wu0", [128, 8, FF], BF16)
        Wo0 = kb.sb(stF, "Wo0", [128, 8, D], BF16)
        stgF = None
        pre = kb.load_w_thunks(stgF, e_w_out, Wo0, Wo0.b, 8, 0, D, CH=1024) + \
            kb.load_w_thunks(stgF, w_gate[0], Wg0, Wg0.b, 8, 0, FF, CH=1408) + \
            kb.load_w_thunks(stgF, w_up[0], Wu0, Wu0.b, 8, 0, FF, CH=1408)
        with ExitStack() as st:
          if 2 not in skip:
            Qts = Rot([kb.sb(st, "Qt", [128, SEQ], BF16) for _ in range(2)])
            Kts = Rot([kb.sb(st, "Kt", [128, SEQ], BF16) for _ in range(2)])
            Vts = Rot([kb.sb(st, "Vt", [128, 32, 128], BF16) for _ in range(2)])
            Ps = Rot([kb.sb(st, "P", [128, 1024], BF16) for _ in range(6)])
            rcs = Rot([kb.sb(st, "rc", [128, 512], F32) for _ in range(2)])
            lcs = Rot([kb.sb(st, "lc", [128, 512], F32) for _ in range(2)])
            tcs = Rot([kb.sb(st, "tc", [128, 512], F32) for _ in range(4)])
            ofs = Rot([kb.sb(st, "of", [128, 512], F32) for _ in range(2)])
            sqs = Rot([kb.sb(st, "sq", [128, 512], BF16) for _ in range(2)])
            rss = Rot([kb.sb(st, "rs", [128, 512], F32) for _ in range(2)])
            obs = Rot([kb.sb(st, "ob", [128, 512], BF16) for _ in range(3)])
            lv = kb.sb(st, "lv", [128, 4, 64], F32)
            ltmp = kb.sb(st, "ltmp", [128, 2, 64], F32)
            lsum = kb.sb(st, "lsum", [128, 2], F32)
            Ss = Rot([kb.ps(st, "S", [128, 1024]) for _ in range(2)])
            Os = Rot([kb.ps(st, "O", [128, 512]) for _ in range(2)])
            L01 = kb.ps(st, "L01", [128, 512])
            X = kb.ps(st, "X", [128, 512])
            oss = Rot([kb.sb(st, "os", [128, 512], F32) for _ in range(4)])
            hooks = {}
            lam_init = 0.8 - 0.6 * math.exp(-0.3 * 0)
            kb.dma("sync", lv.t[:], lamv, (), [lv.b], lv.b)
            kb.tt("vector", ltmp.t[:, 0, :], lv.t[:, 0, :], lv.t[:, 1, :], ALU.mult, [lv.b], [ltmp.b])
            kb.tt("vector", ltmp.t[:, 1, :], lv.t[:, 2, :], lv.t[:, 3, :], ALU.mult, [lv.b], [ltmp.b])
            S.op("vector", lambda e: e.reduce_sum(out=lsum.t[:, 0:1], in_=ltmp.t[:, 0, :], axis=AX.X), [ltmp.b], [lsum.b])
            S.op("vector", lambda e: e.reduce_sum(out=lsum.t[:, 1:2], in_=ltmp.t[:, 1, :], axis=AX.X), [ltmp.b], [lsum.b])
            kb.act(lsum.t[:], lsum.t[:], AF.Exp, [lsum.b], [lsum.b])
            kb.tt("vector", lam_t.t[:, 0:1], lsum.t[:, 1:2], lsum.t[:, 0:1], ALU.subtract, [lsum.b], [lam_t.b])
            kb.ts("vector", lam_t.t[:, 0:1], lam_t.t[:, 0:1], -lam_init, ALU.add, [lam_t.b], [lam_t.b])
            kb.ts("vector", lam_t.t[:, 1:2], sv.t[:, 3:4], 1.0 - lam_init, ALU.mult, [sv.b], [lam_t.b])


            def load_head_tiles(idx):
                return Qts.next(), Kts.next(), Vts.next()

            def load_head(idx, Qt, Kt, Vt):
                if idx < 8:
                    h = idx
                    kb.dma("sync", Qt.t[0:96, :], qm[h], (), [Qt.b], Qt.b)
                    kb.dma("sync", Kt.t[0:96, :], km[h], (), [Kt.b], Kt.b)
                    kb.dma("sync", Vt.t[:], vm[h].rearrange("p (j c) -> p j c", c=128), (), [Vt.b], Vt.b)
                else:
                    h = idx - 8
                    kb.dma("sync", Qt.t[:], qd[h], (), [Qt.b], Qt.b)
                    kb.dma("sync", Kt.t[:], kd[h], (), [Kt.b], Kt.b)
                    kb.dma("sync", Vt.t[:], vd[h].rearrange("p (j c) -> p j c", c=128), (), [Vt.b], Vt.b)

            heads = [load_head_tiles(i) for i in range(12)]
            jobs = []
            for idx in range(12):
                Qt, Kt, Vt = heads[idx]
                for qb in range(8):
                    q0 = qb * 512

                    def on_start(idx=idx, qb=qb):
                        if qb == 0:
                            if idx == 0:
                                pass
                            if idx + 1 < 12:
                                load_head(idx + 1, *heads[idx + 1])
                        if pre and idx >= 1:
                            pre.pop(0)()

                    if idx < 8:
                        h = idx
                        O = Os.next()

                        def epi_mla(O=O, h=h, q0=q0):
                            rc = rcs.next(); ob = obs.next()
                            kb.recip(rc.t[64:128, :], O.t[64:128, :], [O.b], [rc.b])
                            kb.tt("vector", ob.t[0:64, :], O.t[0:64, :], rc.t[64:128, :], ALU.mult, [O.b, rc.b], [ob.b])
                            kb.dma("gpsimd", aT[h * 64:(h + 1) * 64, q0:q0 + 512], ob.t[0:64, :], [ob.b], (), ob.b)

                        jobs.append(core_job(lambda kc, Kt=Kt: Kt.t[0:96, kc * 128:(kc + 1) * 128], Qt.t[0:96, q0:q0 + 512],
                                             lambda kc, Vt=Vt: Vt.t[:, kc, :], Vt.b, [Kt.b, Qt.b], 96 ** -0.5, Ss, Ps, O, None, epi_mla, on_start))
                    else:
                        h = idx - 8
                        O0 = Os.next(); O1 = Os.next()

                        def epi_diff(O0=O0, O1=O1, h=h, q0=q0):
                            o0s = oss.next(); o1s = oss.next(); lc = lcs.next()
                            kb.cp("vector", o0s.t[:], O0.t[:], [O0.b], [o0s.b])
                            kb.cp("vector", o1s.t[:], O1.t[:], [O1.b], [o1s.b])
                            kb.cp("vector", lc.t[:], L01.t[:], [L01.b], [lc.b])
                            box = []

                            def stage_bc(c, osb):
                                rc = rcs.next(); tc = tcs.next()
                                kb.mm(X.t[:], sel.t[64 * c:64 * c + 64, :], lc.t[64 * c:64 * c + 64, :], True, True, [sel.b, lc.b], [X.b])
                                kb.recip(rc.t[:], X.t[:], [X.b], [rc.b])
                                kb.tt("vector", tc.t[:], osb.t[:], rc.t[:], ALU.mult, [osb.b, rc.b], [tc.b])
                                box.append(tc)

                            def stage_d():
                                of = ofs.next(); sq = sqs.next(); rs = rss.next(); ob = obs.next()
                                kb.stt(of.t[:], box[1].t[:], lam_t.t[:, 0:1], box[0].t[:], ALU.mult, ALU.add, [box[0].b, box[1].b, lam_t.b], [of.b])
                                kb.tt("gpsimd", sq.t[:], of.t[:], of.t[:], ALU.mult, [of.b], [sq.b])
                                kb.mm(X.t[:], ones.t[:], sq.t[:], True, True, [ones.b, sq.b], [X.b])
                                kb.act(rs.t[:], X.t[:], AF.Ln, [X.b], [rs.b], scale=1.0 / 128, bias=EPS)
                                kb.act(rs.t[:], rs.t[:], AF.Exp, [rs.b], [rs.b], scale=-0.5)
                                kb.stt(ob.t[:], of.t[:], lam_t.t[:, 1:2], rs.t[:], ALU.mult, ALU.mult, [of.b, lam_t.b, rs.b], [ob.b])
                                kb.dma("gpsimd", aT[512 + h * 128:512 + (h + 1) * 128, q0:q0 + 512], ob.t[:], [ob.b], (), ob.b)

                            hooks.setdefault(2, []).append(lambda: stage_bc(0, o0s))
                            hooks.setdefault(8, []).append(lambda: stage_bc(1, o1s))
                            hooks.setdefault(16, []).append(stage_d)

                        jobs.append(diff_job(Kt, Qt, q0, Vt, 64 ** -0.5, Ss, Ps, O0, O1, L01, epi_diff, on_start))
            load_head(0, *heads[0])
            run_jobs(jobs, hooks)
            for kp in sorted(hooks):
                for fn in hooks[kp]:
                    fn()
            hooks.clear()
            while pre:
                pre.pop(0)()
            S.emit()
        if stop_after <= 2:
            return nc

        def outproj_phase(Wo, a_dram, hin, hout, extra):
            with ExitStack() as st:
                T = 256
                for t in extra:
                    t()
                hbs = Rot([kb.sb(st, "hb", [128, 8, T], F32, 8) for _ in range(2)])
                ats = Rot([kb.sb(st, "at", [128, 8, T], BF16) for _ in range(2)])
                pps = Rot([kb.ps(st, "pp", [128, 512]) for _ in range(4)])
                hv = cv(hin); av = cv(a_dram); ov = cv(hout)
                def load(tb):
                    hb = hbs.next(); at = ats.next()
                    kb.dma("sync", at.t[:], av[:, :, tb * T:(tb + 1) * T], (), [at.b], at.b)
                    kb.dma("sync", hb.t[:], hv[:, :, tb * T:(tb + 1) * T], (), [hb.b] + hb.cb, hb.b)
                    return hb, at

                nxt = load(0)
                for tb in range(SEQ // T):
                    t0 = tb * T
                    hb, at = nxt
                    if tb + 1 < SEQ // T:
                        nxt = load(tb + 1)
                    for o in range(8):
                        pp = pps.next()
                        for k in range(8):
                            kb.mm(pp.t[:, 0:T], Wo.t[:, k, o * 128:(o + 1) * 128], at.t[:, k, :], k == 0, k == 7, [Wo.b, at.b], [pp.b])
                        kb.tt("vector", hb.t[:, o, :], hb.t[:, o, :], pp.t[:, 0:T], ALU.add, [hb.b, pp.b], [hb.cb[o]])
                    kb.dma("sync", ov[:, :, t0:t0 + T], hb.t[:], [hb.b] + hb.cb, (), hb.b)
                S.emit()

        while pre:
            pre.pop(0)()
        Wd0 = kb.sb(stF, "Wd0", [128, NM, D], BF16)
        wd_thunks = kb.load_w_thunks(stgF, w_down[0], Wd0, Wd0.b, NM, 0, D, CH=1024)
        if stop_after <= 3:
            return nc

        def ffn_phase(layer, gcol, hin, hout, final, weights=None, w16=None, attn=None, extra=()):
            with ExitStack() as st:
                T = 256
                GRP = ((0, 6), (6, 12), (12, 17), (17, 22))
                grp_of = [gi for gi, (g0, g1) in enumerate(GRP) for _ in range(g0, g1)]
                if weights is not None:
                    Wg, Wu, Wd = weights
                    wgb = [Wg.b] * 4
                    wub = [Wu.b] * 4
                else:
                  wgb = [Buf("wg%d_%d" % (layer, i)) for i in range(4)]
                  wub = [Buf("wu%d_%d" % (layer, i)) for i in range(4)]
                  Wg = kb.sb(st, "Wg", [128, 8, FF], BF16)
                  Wu = kb.sb(st, "Wu", [128, 8, FF], BF16)
                  Wd = kb.sb(st, "Wd", [128, NM, D], BF16)
                  if w16 is None:
                      for gi, (g0, g1) in enumerate(GRP):
                          kb.load_w(None, w_gate[layer], Wg, wgb[gi], 8, g0 * 128, g1 * 128, dst_c0=g0 * 128)
                          kb.load_w(None, w_up[layer], Wu, wub[gi], 8, g0 * 128, g1 * 128, dst_c0=g0 * 128)
                      kb.load_w(None, w_down[layer], Wd, Wd.b, NM, 0, D, CH=1024)
                  else:
                      g16, u16, d16 = w16
                      g16v = g16.rearrange("(k p) c -> p k c", p=128)
                      u16v = u16.rearrange("(k p) c -> p k c", p=128)
                      d16v = d16.rearrange("(m p) c -> p m c", p=128)
                      for gi, (g0, g1) in enumerate(GRP):
                          kb.dma("scalar", Wg.t[:, :, g0 * 128:g1 * 128], g16v[:, :, g0 * 128:g1 * 128], (), [wgb[gi]], wgb[gi])
                          kb.dma("scalar", Wu.t[:, :, g0 * 128:g1 * 128], u16v[:, :, g0 * 128:g1 * 128], (), [wub[gi]], wub[gi])
                      for m0 in range(0, NM, 6):
                          m1 = min(NM, m0 + 6)
                          kb.dma("scalar", Wd.t[:, m0:m1, :], d16v[:, m0:m1, :], (), [Wd.b], Wd.b, fill=True)
                hbs = Rot([kb.sb(st, "hb", [128, 8, T], F32, 8) for _ in range(2)])
                hns = Rot([kb.sb(st, "hn", [128, 8, T], BF16) for _ in range(2)])
                acs = Rot([kb.sb(st, "ac", [128, NM, T], BF16, NM) for _ in range(1 if attn else 2)])
                ats = Rot([kb.sb(st, "at", [128, 8, T], BF16) for _ in range(2)]) if attn else None
                for t in extra:
                    t()
                sqs = Rot([kb.sb(st, "sq", [128, T], BF16) for _ in range(3)])
                sgs = Rot([kb.sb(st, "sg", [128, T], F32) for _ in range(3)])
                rss = Rot([kb.sb(st, "rs", [128, T], F32) for _ in range(2)])
                ofs = Rot([kb.sb(st, "of", [128, 8, T], F32) for _ in range(1)]) if final else None
                pgs = Rot([kb.ps(st, "pg", [128, 512]) for _ in range(4)])
                pos = Rot([kb.ps(st, "po", [128, 512]) for _ in range(2)])
                ssqs = Rot([kb.ps(st, "ssq", [128, 512]) for _ in range(2)])
                hv = cv(hin); ov = cv(hout)
                nblk = SEQ // T

                def load(tb):
                    hb = hbs.next()
                    if attn:
                        at = ats.next()
                        kb.dma("sync", at.t[:], cv(attn[1])[:, :, tb * T:(tb + 1) * T], (), [at.b], at.b)
                        hb.at = at
                    kb.dma("sync", hb.t[:], hv[:, :, tb * T:(tb + 1) * T], (), [hb.b] + hb.cb, hb.b)
                    return hb

                def pre_norm(hb):
                    if not attn:
                        return
                    Wo_, at = attn[0], hb.at
                    for o in range(8):
                        pp = pos.next()
                        for k in range(8):
                            kb.mm(pp.t[:, 0:T], Wo_.t[:, k, o * 128:(o + 1) * 128], at.t[:, k, :], k == 0, k == 7, [Wo_.b, at.b], [pp.b])
                        kb.tt("vector", hb.t[:, o, :], hb.t[:, o, :], pp.t[:, 0:T], ALU.add, [hb.b, pp.b], [hb.cb[o]])

                nxt = load(0)
                hn_next = hns.next()
                pre_norm(nxt)
                norm_block(st, nxt, hn_next, T, gcol, sqs, ssqs.next(), rss.next())
                for tb in range(nblk):
                    t0 = tb * T
                    hb = nxt
                    hn = hn_next
                    ac = acs.next()
                    if tb + 1 < nblk:
                        nxt = load(tb + 1)
                    for m in range(NM):
                        pg = pgs.next()
                        for k in range(8):
                            kb.mm(pg.t[:, 0:T], Wg.t[:, k, m * 128:(m + 1) * 128], hn.t[:, k, :], k == 0, k == 7, [wgb[grp_of[m]], hn.b], [pg.b])
                        for k in range(8):
                            kb.mm(pg.t[:, T:2 * T], Wu.t[:, k, m * 128:(m + 1) * 128], hn.t[:, k, :], k == 0, k == 7, [wub[grp_of[m]], hn.b], [pg.b])
                        sg = sgs.next()
                        kb.act(sg.t[:], pg.t[:, 0:T], AF.Silu, [pg.b], [sg.b])
                        kb.tt("vector", ac.t[:, m, :], sg.t[:], pg.t[:, T:2 * T], ALU.mult, [sg.b, pg.b], [ac.cb[m]])
                    if tb + 1 < nblk:
                        hn_next = hns.next()
                        pre_norm(nxt)
                        norm_block(st, nxt, hn_next, T, gcol, sqs, ssqs.next(), rss.next())
                    for o in range(8):
                        po = pos.next()
                        for m in range(NM):
                            kb.mm(po.t[:, 0:T], Wd.t[:, m, o * 128:(o + 1) * 128], ac.t[:, m, :], m == 0, m == NM - 1, [Wd.b, ac.cb[m]], [po.b])
                        kb.tt("vector", hb.t[:, o, :], hb.t[:, o, :], po.t[:, 0:T], ALU.add, [hb.b, po.b], [hb.cb[o]])
                    if not final:
                        kb.dma("sync", ov[:, :, t0:t0 + T], hb.t[:], [hb.b] + hb.cb, (), hb.b)
                    else:
                        of = ofs.next()
                        ssq = ssqs.next(); rs = rss.next()
                        for c in range(8):
                            sq = sqs.next()
                            kb.act(sq.t[:, 0:T], hb.t[:, c, :], AF.Square, [hb.cb[c]], [sq.b])
                            kb.mm(ssq.t[:, 0:T], ones.t[:], sq.t[:, 0:T], c == 0, c == 7, [ones.b, sq.b], [ssq.b])
                        kb.act(rs.t[:, 0:T], ssq.t[:, 0:T], AF.Ln, [ssq.b], [rs.b], scale=1.0 / D, bias=EPS)
                        kb.act(rs.t[:, 0:T], rs.t[:, 0:T], AF.Exp, [rs.b], [rs.b], scale=-0.5)
                        for c in range(8):
                            kb.stt(of.t[:, c, :], hb.t[:, c, :], gn.t[:, 32 + c:33 + c], rs.t[:, 0:T], ALU.mult, ALU.mult,
                                   [hb.cb[c], gn.b, rs.b], [of.b])
                        kb.dma("sync", ov[:, :, t0:t0 + T], of.t[:], [of.b], (), of.b)
                S.emit()

        if 4 not in skip:
            ffn_phase(0, 8, xT, h2T, False, weights=(Wg0, Wu0, Wd0), attn=(Wo0, aT), extra=wd_thunks)
        stF.close()
        if stop_after <= 4:
            return nc

        with ExitStack() as st:
            T = 512
            Kt = [kb.sb(st, "Kt", [128, SEQ], BF16) for _ in range(2)]
            Vt = [kb.sb(st, "Vt", [128, 32, 128], BF16) for _ in range(2)]
            h2v = cv(h2T)
            Wq = kb.sb(st, "Wq", [128, 8, D], BF16)
            Wo = kb.sb(st, "Wo2", [128, 8, D], BF16)
            stg6 = None
            pre6 = kb.load_w_thunks(stg6, o_w_qkv, Wq, Wq.b, 8, 0, 1024, CH=1024) + \
                kb.load_w_thunks(stg6, o_w_out, Wo, Wo.b, 8, 0, D, CH=1024)
            with ExitStack() as st5:
                Wk = kb.sb(st5, "Wk", [128, 8, 256], BF16)
                abfs = Rot([kb.sb(st5, "abf", [128, T], BF16) for _ in range(2)])
                Wv = kb.sb(st5, "Wv", [128, 8, 256], BF16)
                kb.load_w(None, o_w_qkv, Wk, Wk.b, 8, 1024, 1280)
                kb.load_w(None, o_w_qkv, Wv, Wv.b, 8, 1280, 1536)
                hbs = Rot([kb.sb(st5, "hb", [128, 8, T], F32) for _ in range(2)])
                hns = Rot([kb.sb(st5, "hn", [128, 8, T], BF16) for _ in range(2)])
                tgs = Rot([kb.sb(st5, "tg", [128, 2, T], F32) for _ in range(2)])
                sqs = Rot([kb.sb(st5, "sq", [128, T], BF16) for _ in range(3)])
                rss = Rot([kb.sb(st5, "rs", [128, T], F32) for _ in range(2)])
                t1s = Rot([kb.sb(st5, "t1", [128, T], F32) for _ in range(2)])
                t2s = Rot([kb.sb(st5, "t2", [128, T], F32) for _ in range(2)])
                t3s = Rot([kb.sb(st5, "t3", [128, T], F32) for _ in range(2)])
                pps = Rot([kb.ps(st5, "pp", [128, 512]) for _ in range(6)])
                ssqs = Rot([kb.ps(st5, "ssq", [128, 512]) for _ in range(2)])
                def load5(tb):
                    hb = hbs.next()
                    kb.dma("sync", hb.t[:], h2v[:, :, tb * T:(tb + 1) * T], (), [hb.b], hb.b)
                    return hb

                hb_next = load5(0)
                hn_next = hns.next()
                norm_block(st5, hb_next, hn_next, T, 16, sqs, ssqs.next(), rss.next())
                for tb in range(SEQ // T):
                    t0 = tb * T
                    hn = hn_next
                    tg = tgs.next()
                    kb.dma("sync", tg.t[:], tab_g[:, :, t0:t0 + T].rearrange("a r t -> r a t"), (), [tg.b], tg.b)
                    if tb + 1 < SEQ // T:
                        hb_next = load5(tb + 1)
                    for _ in range(3):
                        if pre6:
                            pre6.pop(0)()
                    for kvh in range(2):
                        A = pps.next(); B = pps.next()
                        for k in range(8):
                            kb.mm(A.t[:], Wk.t[:, k, kvh * 128:(kvh + 1) * 128], hn.t[:, k, :], k == 0, k == 7, [Wk.b, hn.b], [A.b])
                        abf = abfs.next()
                        kb.cp("scalar", abf.t[:], A.t[:], [A.b], [abf.b])
                        kb.mm(B.t[:], PM.t[:, 128:256], abf.t[:], True, True, [PM.b, abf.b], [B.b])
                        rs = rss.next()
                        head_rstd(A, 128, 128, sqs.next(), ssqs.next(), rs)
                        t1 = t1s.next(); t2 = t2s.next()
                        kb.stt(t1.t[:], A.t[:], sv.t[:, 6:7], tg.t[:, 0, :], ALU.mult, ALU.mult, [A.b, sv.b, tg.b], [t1.b])
                        kb.stt(t2.t[:], B.t[:], sv.t[:, 7:8], tg.t[:, 1, :], ALU.mult, ALU.mult, [B.b, sv.b, tg.b], [t2.b])
                        t3 = t3s.next()
                        kb.tt("gpsimd", t3.t[:], t1.t[:], t2.t[:], ALU.add, [t1.b, t2.b], [t3.b])
                        kb.tt("vector", Kt[kvh].t[:, t0:t0 + T], t3.t[:], rs.t[:], ALU.mult, [t3.b, rs.b], [Kt[kvh].b])
                    if tb + 1 < SEQ // T:
                        hn_next = hns.next()
                        norm_block(st5, hb_next, hn_next, T, 16, sqs, ssqs.next(), rss.next())
                    for tt in range(4):
                        A = pps.next()
                        for k in range(8):
                            kb.mm(A.t[:, 0:256], hn.t[:, k, tt * 128:(tt + 1) * 128], Wv.t[:, k, :], k == 0, k == 7, [Wv.b, hn.b], [A.b])
                        for kvh in range(2):
                            kb.cp("scalar" if kvh else "vector", Vt[kvh].t[:, tb * 4 + tt, :], A.t[:, kvh * 128:(kvh + 1) * 128], [A.b], [Vt[kvh].b])
                while pre6:
                    pre6.pop(0)()
                S.emit()
            if stop_after <= 5:
                return nc
            pcb = Buf("precast")
            pre16 = []
            for k in range(8):
                for hf in range(2):
                    for src, dst in ((w_gate[1], wg16), (w_up[1], wu16)):
                        pre16.append(lambda k=k, hf=hf, src=src, dst=dst: kb.dma(
                            "gpsimd", dst[k * 128:(k + 1) * 128, hf * 1408:(hf + 1) * 1408], src[k * 128:(k + 1) * 128, hf * 1408:(hf + 1) * 1408],
                            (), [pcb], pcb, fill=True))
            for m in range(NM):
                pre16.append(lambda m=m: kb.dma("gpsimd", wd16[m * 128:(m + 1) * 128, :], w_down[1][m * 128:(m + 1) * 128, :], (), [pcb], pcb, fill=True))
            with ExitStack() as st6:
                hbs = Rot([kb.sb(st6, "hb", [128, 8, T], F32, 8) for _ in range(2)])
                hns = Rot([kb.sb(st6, "hn", [128, 8, T], BF16) for _ in range(2)])
                tgs = Rot([kb.sb(st6, "tg", [128, 2, T], F32) for _ in range(2)])
                Qts = Rot([kb.sb(st6, "Qt", [128, 8, T], BF16, 8) for _ in range(1)])
                ats = Rot([kb.sb(st6, "at", [128, 8, T], BF16, 8) for _ in range(1)])
                sqs = Rot([kb.sb(st6, "sq", [128, T], BF16) for _ in range(3)])
                rss = Rot([kb.sb(st6, "rs", [128, T], F32) for _ in range(2)])
                t1s = Rot([kb.sb(st6, "t1", [128, T], F32) for _ in range(2)])
                t2s = Rot([kb.sb(st6, "t2", [128, T], F32) for _ in range(2)])
                t3s = Rot([kb.sb(st6, "t3", [128, T], F32) for _ in range(2)])
                rcs = Rot([kb.sb(st6, "rc", [128, T], F32) for _ in range(2)])
                lcs = Rot([kb.sb(st6, "lc", [128, T], F32) for _ in range(2)])
                Ps = Rot([kb.sb(st6, "P", [128, 1024], BF16) for _ in range(6)])
                Ss = Rot([kb.ps(st6, "S", [128, 1024]) for _ in range(2)])
                Os = Rot([kb.ps(st6, "O", [128, 512]) for _ in range(2)])
                Ls = Rot([kb.ps(st6, "L", [128, 512]) for _ in range(2)])
                h3v = cv(h3T)
                hooks6 = {}
                abfs6 = Rot([kb.sb(st6, "abf", [128, T], BF16) for _ in range(2)])
                def load6(qb):
                    hb = hbs.next(); tg = tgs.next()
                    kb.dma("sync", hb.t[:], h2v[:, :, qb * T:(qb + 1) * T], (), [hb.b] + hb.cb, hb.b)
                    kb.dma("sync", tg.t[:], tab_g[:, :, qb * T:(qb + 1) * T].rearrange("a r t -> r a t"), (), [tg.b], tg.b)
                    return hb, tg

                nxt6 = load6(0)
                hn_next6 = [hns.next()]
                norm_block(st6, nxt6[0], hn_next6[0], T, 16, sqs, Ls.next(), rss.next())
                for qb in range(SEQ // T):
                    t0 = qb * T
                    hb, tg = nxt6
                    hn = hn_next6[0]
                    Qt = Qts.next(); at = ats.next()
                    if qb + 1 < SEQ // T:
                        nxt6 = load6(qb + 1)
                    for hq in range(8):
                        A = Os.next(); B = Os.next()
                        for k in range(8):
                            kb.mm(A.t[:], Wq.t[:, k, hq * 128:(hq + 1) * 128], hn.t[:, k, :], k == 0, k == 7, [Wq.b, hn.b], [A.b])
                        abf = abfs6.next()
                        kb.cp("scalar", abf.t[:], A.t[:], [A.b], [abf.b])
                        kb.mm(B.t[:], PM.t[:, 128:256], abf.t[:], True, True, [PM.b, abf.b], [B.b])
                        rs = rss.next()
                        head_rstd(A, 128, 128, sqs.next(), Ls.next(), rs)
                        t1 = t1s.next(); t2 = t2s.next()
                        kb.stt(t1.t[:], A.t[:], sv.t[:, 4:5], tg.t[:, 0, :], ALU.mult, ALU.mult, [A.b, sv.b, tg.b], [t1.b])
                        kb.stt(t2.t[:], B.t[:], sv.t[:, 5:6], tg.t[:, 1, :], ALU.mult, ALU.mult, [B.b, sv.b, tg.b], [t2.b])
                        t3 = t3s.next()
                        kb.tt("gpsimd", t3.t[:], t1.t[:], t2.t[:], ALU.add, [t1.b, t2.b], [t3.b])
                        kb.tt("vector", Qt.t[:, hq, :], t3.t[:], rs.t[:], ALU.mult, [t3.b, rs.b], [Qt.cb[hq]])
                    jobs6 = []
                    for hq in range(8):
                        kvh = hq // 4
                        O = Os.next(); L = Ls.next()

                        def epi6(O=O, L=L, at=at, hq=hq):
                            lc = lcs.next()
                            kb.cp("vector", lc.t[:], L.t[:], [L.b], [lc.b])

                            def epi():
                                rc = rcs.next()
                                kb.mm(L.t[:], sel.t[:], lc.t[:], True, True, [sel.b, lc.b], [L.b])
                                kb.recip(rc.t[:], L.t[:], [L.b], [rc.b])
                                kb.tt("vector", at.t[:, hq, :], O.t[:], rc.t[:], ALU.mult, [O.b, rc.b], [at.cb[hq]])
                            hooks6.setdefault(1, []).append(epi)

                        def on6(hq=hq, qb=qb, nb=nxt6):
                            if pre16:
                                pre16.pop(0)()
                            if hq == 5 and qb + 1 < SEQ // T:
                                hn_next6[0] = hns.next()
                                norm_block(st6, nb[0], hn_next6[0], T, 16, sqs, Ls.next(), rss.next())

                        jobs6.append(core_job(lambda kc, kvh=kvh: Kt[kvh].t[:, kc * 128:(kc + 1) * 128], Qt.t[:, hq, :],
                                              lambda kc, kvh=kvh: Vt[kvh].t[:, kc, :], Vt[kvh].b, [Kt[kvh].b, Qt.cb[hq]], 128 ** -0.5,
                                              Ss, Ps, O, L, epi6, on6))
                    run_jobs(jobs6, hooks6)
                    for kp_ in sorted(hooks6):
                        for fn in hooks6[kp_]:
                            fn()
                    hooks6.clear()
                    for o in range(8):
                        pp = Os.next()
                        for k in range(8):
                            kb.mm(pp.t[:], Wo.t[:, k, o * 128:(o + 1) * 128], at.t[:, k, :], k == 0, k == 7, [Wo.b, at.cb[k]], [pp.b])
                        kb.tt("vector", hb.t[:, o, :], hb.t[:, o, :], pp.t[:], ALU.add, [hb.b, pp.b], [hb.cb[o]])
                    kb.dma("sync", h3v[:, :, t0:t0 + T], hb.t[:], [hb.b] + hb.cb, (), hb.b)
                while pre16:
                    pre16.pop(0)()
                S.emit()
        if stop_after <= 6:
            return nc

        ffn_phase(1, 24, h3T, outT, True, w16=(wg16, wu16, wd16))
    return nc


def _rope_tab(pos, dim, theta):
    inv = (np.float32(theta) ** (-np.arange(0, dim, 2, dtype=np.float32) / np.float32(dim))).astype(np.float32)
    ang = (pos.astype(np.float32)[:, None] * inv[None, :]).astype(np.float32)
    c = np.cos(ang.astype(np.float64)).astype(np.float32)
    s = np.sin(ang.astype(np.float64)).astype(np.float32)
    return np.concatenate([c, c], 1).T.copy(), np.concatenate([-s, s], 1).T.copy()


def _half_perm(n):
    h = n // 2
    return np.concatenate([np.arange(h, n), np.arange(0, h)])


def _tables():
    pos = np.arange(SEQ)
    c32, s32 = _rope_tab(pos, 32, 500000.0)
    tab_m = np.zeros((2, 96, SEQ), np.float32)
    tab_m[0, :64] = 1.0
    tab_m[0, 64:] = c32
    tab_m[1, 64:] = s32
    tab_kr = np.stack([c32, s32]).astype(np.float32)
    c16, s16 = _rope_tab(pos, 16, 500000.0)
    tab_d = np.zeros((2, 128, SEQ), np.float32)
    tab_d[0] = 1.0
    for g in range(2):
        tab_d[0, g * 64:g * 64 + 16] = c16
        tab_d[1, g * 64:g * 64 + 16] = s16
    cr, sr = _rope_tab(pos // 64, 64, 10000.0)
    cc, sc = _rope_tab(pos % 64, 64, 10000.0)
    tab_g = np.stack([np.concatenate([cr, cc], 0), np.concatenate([sr, sc], 0)]).astype(np.float32)
    return tab_m, tab_kr, tab_d, tab_g


def _col128(v):
    return np.ascontiguousarray(np.asarray(v, np.float32).reshape(-1, 128).T)


def _pms(dp, gp):
    m = np.zeros((128, 256), np.float32)
    m[dp, np.arange(128)] = 1.0
    m[gp, 128 + np.arange(128)] = 1.0
    return m


def _sel():
    s = np.zeros((128, 128), np.float32)
    s[0::32, :] = 1.0
    return s


_CACHE = {}


def kernel(x, e_attn_norm, e_w_in, e_q_norm, e_w_uq, e_kv_norm, e_w_ukv,
           e_lambda_q1, e_lambda_k1, e_lambda_q2, e_lambda_k2, e_subln, e_w_out,
           o_attn_norm, o_w_qkv, o_q_norm, o_k_norm, o_w_out,
           ffn_norm, w_gate, w_up, w_down, final_norm, _stop_after=99, _dbg=False, _ncores=8, _skip=()):
    f = lambda a: np.ascontiguousarray(np.asarray(a, dtype=np.float32))
    x = f(x)
    tab_m, tab_kr, tab_d, tab_g = _tables()
    gains = np.concatenate([_col128(f(e_attn_norm)[0]), _col128(f(ffn_norm)[0]), _col128(f(o_attn_norm)[0]),
                            _col128(f(ffn_norm)[1]), _col128(f(final_norm))], axis=1)
    p32 = _half_perm(32); p16 = _half_perm(16); p64 = _half_perm(64)
    w_in = f(e_w_in)[0]
    kr_cols = 320 + p32
    d_perm = np.arange(512)
    for g in range(8):
        d_perm[g * 64:g * 64 + 16] = g * 64 + p16
    w_in_p = np.concatenate([w_in[:, kr_cols], w_in[:, 352 + d_perm], w_in[:, 864 + d_perm]], axis=1)
    uq = f(e_w_uq)[0]
    uq_perm = np.arange(768)
    for h in range(8):
        uq_perm[h * 96 + 64:h * 96 + 96] = h * 96 + 64 + p32
    uq_p = uq[:, uq_perm]
    g_perm = np.concatenate([p64, 64 + p64])
    qkv = f(o_w_qkv)[0]
    qk_perm = np.arange(1280)
    for h in range(10):
        qk_perm[h * 128:(h + 1) * 128] = h * 128 + g_perm
    qkv_p = qkv[:, qk_perm]
    gq = f(o_q_norm)[0]; gk = f(o_k_norm)[0]
    qn = f(e_q_norm)[0]
    svec = np.zeros((128, 8), np.float32)
    svec[:, 0] = qn[:128]
    svec[:64, 1] = qn[128:192]
    svec[:, 2] = f(e_kv_norm)[0]
    svec[:, 3] = f(e_subln)[0]
    svec[:, 4] = gq; svec[:, 5] = gq[g_perm]; svec[:, 6] = gk; svec[:, 7] = gk[g_perm]
    lamv = np.stack([f(e_lambda_q1)[0], f(e_lambda_k1)[0], f(e_lambda_q2)[0], f(e_lambda_k2)[0]])[None].repeat(128, 0)
    common = {
        "gains": f(gains), "svec": svec, "lamv": f(lamv), "sel": _sel(), "pms": _pms(d_perm[:128], g_perm),
        "e_w_in": w_in, "e_w_in_p": f(w_in_p), "e_w_uq": uq, "e_w_uq_p": f(uq_p), "e_w_ukv": f(e_w_ukv)[0],
        "e_w_out": f(e_w_out)[0], "o_w_qkv": qkv, "o_w_qkv_p": f(qkv_p), "o_w_out": f(o_w_out)[0],
        "w_gate": f(w_gate), "w_up": f(w_up), "w_down": f(w_down),
        "tab_m": tab_m, "tab_kr": tab_kr, "tab_d": tab_d, "tab_g": tab_g,
    }
    key = (_stop_after, _dbg, tuple(_skip))
    if key not in _CACHE:
        _CACHE[key] = build_program(_stop_after, _dbg, tuple(_skip))
    nc = _CACHE[key]
    in_maps = []
    for b in range(_ncores):
        m = dict(common)
        m["xT"] = np.ascontiguousarray(x[b].T)
        in_maps.append(m)
    res = run_bass_kernel_spmd(nc, in_maps, core_ids=list(range(_ncores)))
    if _dbg:
        return res
    out = np.stack([np.asarray(r["outT"]).T for r in res.results], axis=0)
    return np.ascontiguousarray(out.astype(np.float32))
```

```python
import math
from contextlib import ExitStack
import numpy as np
import concourse.bass as bass
import concourse.mybir as mybir
from concourse.bass_utils import run_bass_kernel_spmd

F32 = mybir.dt.float32
BF16 = mybir.dt.bfloat16
AF = mybir.ActivationFunctionType
ALU = mybir.AluOpType
AX = mybir.AxisListType

ENGINES = ("tensor", "vector", "scalar", "gpsimd", "sync")

SEQ = 4096
D = 1024
EPS = 1e-6
FF = 2816
NM = FF // 128


class Buf:
    __slots__ = ("name", "last_w", "readers", "sem", "excl")

    def __init__(self, name):
        self.name = name
        self.excl = False
        self.last_w = None
        self.readers = {}
        self.sem = None


class Op:
    __slots__ = ("eng", "fn", "deps", "needed", "sig", "dma_buf")

    def __init__(self, eng, fn):
        self.eng = eng
        self.fn = fn
        self.deps = []
        self.needed = False
        self.sig = None
        self.dma_buf = None


class Sched:
    def __init__(self, nc, stack, n_dma_sems=18, n_sw_sems=16):
        self.nc = nc
        self.esem = {e: stack.enter_context(nc.semaphore("s_" + e)) for e in ENGINES}
        self.ecnt = {e: 0 for e in ENGINES}
        self.pool = [[stack.enter_context(nc.semaphore("d%d" % i)), 0] for i in range(n_dma_sems)]
        self.free = {"sync": list(self.pool), "gpsimd": [[stack.enter_context(nc.semaphore("w%d" % i)), 0] for i in range(n_sw_sems)]}
        self.free["scalar"] = self.free["sync"]
        self.known = {e: {} for e in ENGINES}
        self.reset()

    def reset(self):
        self.ops = {e: [] for e in ENGINES}
        self.all_ops = []
        self.phase_bufs = []

    def op(self, eng, fn, reads=(), writes=(), dma_buf=None, fill=False):
        o = Op(eng, fn)
        o.dma_buf = dma_buf
        writes = list(writes) + [b for b in reads if b.excl]
        reads = [b for b in reads if not b.excl]
        deps = {}
        for b in reads:
            if b.last_w is not None:
                deps[id(b.last_w)] = b.last_w
        for b in writes:
            if b.last_w is not None and not (fill and b.last_w.dma_buf is dma_buf and b.last_w.eng == eng):
                deps[id(b.last_w)] = b.last_w
            for r in b.readers.values():
                deps[id(r)] = r
        key = ("dma", id(dma_buf)) if dma_buf is not None else eng
        for b in reads:
            b.readers[key] = o
        for b in writes:
            b.last_w = o
            b.readers = {}
        o.deps = list(deps.values())
        for d in o.deps:
            d.needed = True
        if dma_buf is not None:
            o.needed = True
            if dma_buf.sem is None:
                dma_buf.sem = self.free[eng].pop()
                self.phase_bufs.append((dma_buf, eng))
        self.ops[eng].append(o)
        self.all_ops.append(o)
        return o

    def barrier(self):
        last = {}
        for o in self.all_ops:
            if o.fn is None:
                continue
            key = ("dma", id(o.dma_buf)) if o.dma_buf is not None else o.eng
            last[key] = o
        lasts = list(last.values())
        for e in ENGINES:
            o = Op(e, None)
            o.deps = list(lasts)
            for d in o.deps:
                d.needed = True
            self.ops[e].append(o)
            self.all_ops.append(o)

    def emit(self):
        self.barrier()
        nc = self.nc
        for o in self.all_ops:
            if o.fn is None:
                continue
            if o.dma_buf is not None:
                s = o.dma_buf.sem
                s[1] += 16
                o.sig = (s[0], s[1], 16)
            elif o.needed:
                self.ecnt[o.eng] += 1
                o.sig = (self.esem[o.eng], self.ecnt[o.eng], 1)
        ops = self.ops
        known_all = self.known

        def run(e):
            def body(eng):
                known = known_all[e]
                for o in ops[e]:
                    need = {}
                    for d in o.deps:
                        if d.sig is None:
                            continue
                        if d.eng == e and e == "tensor" and d.dma_buf is None:
                            continue
                        s, v, _ = d.sig
                        k = id(s)
                        if known.get(k, 0) >= v:
                            continue
                        if k not in need or need[k][1] < v:
                            need[k] = (s, v)
                    for k, (s, v) in need.items():
                        eng.wait_ge(s, v)
                        known[k] = v
                    if o.fn is None:
                        continue
                    ins = o.fn(eng)
                    if o.sig is not None:
                        ins.then_inc(o.sig[0], o.sig[2])
            return body

        with nc.Block() as block:
            block.tensor(run("tensor"))
            block.vector(run("vector"))
            block.scalar(run("scalar"))
            block.gpsimd(run("gpsimd"))
            block.sync(run("sync"))
        for b, q in self.phase_bufs:
            self.free[q].append(b.sem)
            b.sem = None
        self.reset()


class Tl:
    def __init__(self, t, name, nchunks=0):
        self.t = t
        self.b = Buf(name)
        self.cb = [Buf("%s_%d" % (name, i)) for i in range(nchunks)]


class Rot:
    def __init__(self, tiles):
        self.tiles = tiles
        self.i = 0

    def next(self):
        t = self.tiles[self.i % len(self.tiles)]
        self.i += 1
        return t


class KB:
    def __init__(self, nc, stack):
        self.nc = nc
        self.S = Sched(nc, stack)
        self.uid = 0

    def sb(self, st, name, shape, dt, nchunks=0):
        self.uid += 1
        nm = "%s_%d" % (name, self.uid)
        return Tl(st.enter_context(self.nc.sbuf_tensor(nm, shape, dt)), nm, nchunks)

    def ps(self, st, name, shape, dt=F32):
        self.uid += 1
        nm = "%s_%d" % (name, self.uid)
        t = Tl(st.enter_context(self.nc.psum_tensor(nm, shape, dt)), nm)
        t.b.excl = True
        return t

    def mm(self, out, lhsT, rhs, start, stop, reads, writes, tp=None):
        if tp is None:
            self.S.op("tensor", lambda e: e.matmul(out, lhsT=lhsT, rhs=rhs, start=start, stop=stop), reads, writes)
        else:
            self.S.op("tensor", lambda e: e.matmul(out, lhsT=lhsT, rhs=rhs, start=start, stop=stop, tile_position=tp), reads, writes)

    def act(self, out, in_, func, reads, writes, scale=1.0, bias=None):
        if bias is None:
            self.S.op("scalar", lambda e: e.activation(out=out, in_=in_, func=func, scale=scale), reads, writes)
        else:
            self.S.op("scalar", lambda e: e.activation(out=out, in_=in_, func=func, scale=scale, bias=bias), reads, writes)

    def tt(self, eng, out, in0, in1, op, reads, writes):
        self.S.op(eng, lambda e: e.tensor_tensor(out=out, in0=in0, in1=in1, op=op), reads, writes)

    def stt(self, out, in0, scalar, in1, op0, op1, reads, writes):
        self.S.op("vector", lambda e: e.scalar_tensor_tensor(out=out, in0=in0, scalar=scalar, in1=in1, op0=op0, op1=op1), reads, writes)

    def ts(self, eng, out, in0, s1, op0, reads, writes, s2=None, op1=None):
        if op1 is None:
            self.S.op(eng, lambda e: e.tensor_scalar(out=out, in0=in0, scalar1=s1, scalar2=None, op0=op0), reads, writes)
        else:
            self.S.op(eng, lambda e: e.tensor_scalar(out=out, in0=in0, scalar1=s1, scalar2=s2, op0=op0, op1=op1), reads, writes)

    def cp(self, eng, out, in_, reads, writes):
        if eng == "scalar":
            self.S.op(eng, lambda e: e.activation(out=out, in_=in_, func=AF.Copy), reads, writes)
        else:
            self.S.op(eng, lambda e: e.tensor_copy(out=out, in_=in_), reads, writes)

    def recip(self, out, in_, reads, writes):
        self.S.op("vector", lambda e: e.reciprocal(out=out, in_=in_), reads, writes)

    def memset(self, eng, ap, val, writes):
        self.S.op(eng, lambda e: e.memset(ap, val), (), writes)

    def dma(self, q, out, in_, reads, writes, buf, fill=False):
        self.S.op(q, lambda e: e.dma_start(out=out, in_=in_), reads, writes, dma_buf=buf, fill=fill)

    def load_w_thunks(self, stg, dram, W, wbuf, nk, c0, c1, dst_c0=0, rows_last=128, engs=None, CH=2048):
        th = []
        for k in range(nk):
            rows = 128 if k < nk - 1 else rows_last
            for cc in range(c0, c1, CH):
                ce = min(c1, cc + CH)

                def one(k=k, rows=rows, cc=cc, ce=ce):
                    d0 = dst_c0 + cc - c0
                    self.dma("gpsimd", W.t[0:rows, k, d0:d0 + ce - cc], dram[k * 128:k * 128 + rows, cc:ce], (), [wbuf], wbuf, fill=True)
                th.append(one)
        return th

    def load_w(self, *a, **kw):
        for t in self.load_w_thunks(*a, **kw):
            t()


def build_program(stop_after=99, dbg=False, skip=()):
    nc = bass.Bass("TRN2", target_bir_lowering=False)

    def din(name, shape, dt=F32):
        return nc.dram_tensor(name, shape, dt, kind="ExternalInput").ap()

    def dscr(name, shape, dt):
        kind = "ExternalOutput" if dbg else "Internal"
        return nc.dram_tensor(name, shape, dt, kind=kind).ap()

    xT = din("xT", [D, SEQ])
    gains = din("gains", [128, 40])
    svec = din("svec", [128, 8])
    lamv = din("lamv", [128, 4, 64])
    sel_d = din("sel", [128, 128])
    pms_d = din("pms", [128, 256])
    e_w_in = din("e_w_in", [D, 1888])
    e_w_in_p = din("e_w_in_p", [D, 1056])
    e_w_uq = din("e_w_uq", [192, 768])
    e_w_uq_p = din("e_w_uq_p", [192, 768])
    e_w_ukv = din("e_w_ukv", [128, 1024])
    e_w_out = din("e_w_out", [D, D])
    o_w_qkv = din("o_w_qkv", [D, 1536])
    o_w_qkv_p = din("o_w_qkv_p", [D, 1280])
    o_w_out = din("o_w_out", [D, D])
    w_gate = din("w_gate", [2, D, FF])
    w_up = din("w_up", [2, D, FF])
    w_down = din("w_down", [2, FF, D])
    tab_m = din("tab_m", [2, 96, SEQ])
    tab_kr = din("tab_kr", [2, 32, SEQ])
    tab_d = din("tab_d", [2, 128, SEQ])
    tab_g = din("tab_g", [2, 128, SEQ])
    outT = nc.dram_tensor("outT", [D, SEQ], F32, kind="ExternalOutput").ap()

    qm = dscr("qm", [8, 96, SEQ], BF16)
    km = dscr("km", [8, 96, SEQ], BF16)
    vm = dscr("vm", [8, 128, SEQ], BF16)
    qd = dscr("qd", [4, 128, SEQ], BF16)
    kd = dscr("kd", [4, 128, SEQ], BF16)
    vd = dscr("vd", [4, 128, SEQ], BF16)
    aT = dscr("aT", [D, SEQ], BF16)
    h1T = dscr("h1T", [D, SEQ], F32)
    h2T = dscr("h2T", [D, SEQ], F32)
    h3T = dscr("h3T", [D, SEQ], F32)
    wg16 = nc.dram_tensor("wg16", [D, FF], BF16, kind="Internal").ap()
    wu16 = nc.dram_tensor("wu16", [D, FF], BF16, kind="Internal").ap()
    wd16 = nc.dram_tensor("wd16", [FF, D], BF16, kind="Internal").ap()

    def cv(ap):
        return ap.rearrange("(c p) t -> p c t", p=128)

    with ExitStack() as gst:
        kb = KB(nc, gst)
        S = kb.S
        ones = kb.sb(gst, "ones", [128, 128], BF16)
        gn = kb.sb(gst, "gains", [128, 40], F32)
        sv = kb.sb(gst, "svec", [128, 8], F32)
        lam_t = kb.sb(gst, "lamt", [128, 8], F32)
        kb.memset("vector", ones.t[:], 1.0, [ones.b])
        kb.dma("sync", gn.t[:], gains, (), [gn.b], gn.b)
        kb.dma("sync", sv.t[:], svec, (), [sv.b], sv.b)
        sel = kb.sb(gst, "sel", [128, 128], F32)
        kb.dma("sync", sel.t[:], sel_d, (), [sel.b], sel.b)
        PM = kb.sb(gst, "PM", [128, 256], BF16)
        kb.dma("gpsimd", PM.t[:], pms_d, (), [PM.b], PM.b)

        def norm_block(st, hb, hn, T, gcol, sq_rot, ssq, rs):
            for c in range(8):
                sq = sq_rot.next()
                kb.act(sq.t[:, 0:T], hb.t[:, c, :], AF.Square, [hb.b] + hb.cb[c:c + 1], [sq.b])
                kb.mm(ssq.t[:, 0:T], ones.t[:], sq.t[:, 0:T], c == 0, c == 7, [ones.b, sq.b], [ssq.b])
            kb.act(rs.t[:, 0:T], ssq.t[:, 0:T], AF.Ln, [ssq.b], [rs.b], scale=1.0 / D, bias=EPS)
            kb.act(rs.t[:, 0:T], rs.t[:, 0:T], AF.Exp, [rs.b], [rs.b], scale=-0.5)
            for c in range(8):
                kb.stt(hn.t[:, c, :], hb.t[:, c, :], gn.t[:, gcol + c:gcol + c + 1], rs.t[:, 0:T],
                       ALU.mult, ALU.mult, [hb.b, gn.b, rs.b] + hb.cb[c:c + 1], [hn.b])

        def head_rstd(A, rows, nfeat, sq, ssq, rs, extra=None):
            kb.act(sq.t[0:rows, :], A.t[0:rows, :], AF.Square, [A.b], [sq.b])
            if extra is None:
                kb.mm(ssq.t[:], ones.t[0:rows, :], sq.t[0:rows, :], True, True, [ones.b, sq.b], [ssq.b])
            else:
                A2, rows2, sq2 = extra
                kb.act(sq2.t[0:rows2, :], A2.t[0:rows2, :], AF.Square, [A2.b], [sq2.b])
                kb.mm(ssq.t[:], ones.t[0:rows, :], sq.t[0:rows, :], True, False, [ones.b, sq.b], [ssq.b])
                kb.mm(ssq.t[:], ones.t[0:rows2, :], sq2.t[0:rows2, :], False, True, [ones.b, sq2.b], [ssq.b])
            kb.act(rs.t[:], ssq.t[:], AF.Ln, [ssq.b], [rs.b], scale=1.0 / nfeat, bias=EPS)
            kb.act(rs.t[:], rs.t[:], AF.Exp, [rs.b], [rs.b], scale=-0.5)

        if 1 not in skip:
            with ExitStack() as st:
                T = 512
                Win = kb.sb(st, "Win", [128, 8, 1888], BF16)
                Winp = kb.sb(st, "Winp", [128, 8, 32], BF16)
                abfs = Rot([kb.sb(st, "abf", [128, T], BF16) for _ in range(3)])
                Wuq = kb.sb(st, "Wuq", [128, 2, 768], BF16)
                Wuqp = kb.sb(st, "Wuqp", [128, 2, 768], BF16)
                Wukv = kb.sb(st, "Wukv", [128, 1, 1024], BF16)
                stg = None
                Win_b2 = Buf("Win_b2"); Winp_b2 = Buf("Winp_b2")
                kb.load_w(stg, e_w_in, Win, Win.b, 8, 0, 352)
                kb.load_w(stg, e_w_in_p, Winp, Winp.b, 8, 0, 32)
                kb.load_w(stg, e_w_uq, Wuq, Wuq.b, 2, 0, 768, rows_last=64)
                kb.load_w(stg, e_w_uq_p, Wuqp, Wuqp.b, 2, 0, 768, rows_last=64)
                kb.load_w(stg, e_w_ukv, Wukv, Wukv.b, 1, 0, 1024)
                kb.load_w(stg, e_w_in, Win, Win_b2, 8, 352, 1888, dst_c0=352)
                hbs = Rot([kb.sb(st, "hb", [128, 8, T], F32) for _ in range(2)])
                hns = Rot([kb.sb(st, "hn", [128, 8, T], BF16) for _ in range(2)])
                tms = Rot([kb.sb(st, "tm", [96, 2, T], F32) for _ in range(2)])
                tks = Rot([kb.sb(st, "tk", [32, 2, T], F32) for _ in range(2)])
                tds = Rot([kb.sb(st, "td", [128, 2, T], F32) for _ in range(2)])
                sqs = Rot([kb.sb(st, "sq", [128, T], BF16) for _ in range(3)])
                rss = Rot([kb.sb(st, "rs", [128, T], F32) for _ in range(2)])
                t1s = Rot([kb.sb(st, "t1", [128, T], F32) for _ in range(3)])
                t2s = Rot([kb.sb(st, "t2", [128, T], F32) for _ in range(3)])
                obs = Rot([kb.sb(st, "ob", [128, T], BF16) for _ in range(4)])
                obs2 = Rot([kb.sb(st, "ob2", [128, T], BF16) for _ in range(6)])
                cqn0 = kb.sb(st, "cqn0", [128, T], BF16)
                cqn1 = kb.sb(st, "cqn1", [64, T], BF16)
                ckvn = kb.sb(st, "ckvn", [128, T], BF16)
                vts = Rot([kb.sb(st, "vt", [128, 8, 4, 128], BF16) for _ in range(2)])
                vds = Rot([kb.sb(st, "vdt", [128, 4, 4, 128], BF16) for _ in range(2)])
                vmv = vm.rearrange("h p (j c) -> p h j c", c=128)
                vdv = vd.rearrange("h p (j c) -> p h j c", c=128)
                pps = Rot([kb.ps(st, "pp", [128, 512]) for _ in range(6)])
                ssqs = Rot([kb.ps(st, "ssq", [128, 512]) for _ in range(2)])
                for v in vts.tiles:
                    kb.memset("vector", v.t[:, :, :, 64:128], 1.0, [v.b])

                def rope_out(A, B, rows, tc, tsn, tab_b, dst_dram, eng_add, rot=None):
                    t1 = t1s.next(); t2 = t2s.next(); ob = (rot or obs).next()
                    kb.tt("vector", t1.t[0:rows, :], A.t[0:rows, :], tc, ALU.mult, [A.b, tab_b], [t1.b])
                    kb.tt("vector", t2.t[0:rows, :], B.t[0:rows, :], tsn, ALU.mult, [B.b, tab_b], [t2.b])
                    kb.tt(eng_add, ob.t[0:rows, :], t1.t[0:rows, :], t2.t[0:rows, :], ALU.add, [t1.b, t2.b], [ob.b])
                    return ob

                xTv = cv(xT)
                nblk = SEQ // T
                def load1(tb):
                    hb = hbs.next()
                    kb.dma("sync", hb.t[:], xTv[:, :, tb * T:(tb + 1) * T], (), [hb.b], hb.b)
                    return hb

                hb_next = load1(0)
                hn_next = hns.next()
                norm_block(st, hb_next, hn_next, T, 0, sqs, ssqs.next(), rss.next())
                for tb in range(nblk):
                    t0 = tb * T
                    hn = hn_next
                    tm = tms.next(); tk = tks.next(); td = tds.next()
                    kb.dma("sync", tm.t[:], tab_m[:, :, t0:t0 + T].rearrange("a r t -> r a t"), (), [tm.b], tm.b)
                    kb.dma("sync", tk.t[:], tab_kr[:, :, t0:t0 + T].rearrange("a r t -> r a t"), (), [tk.b], tk.b)
                    kb.dma("sync", td.t[:], tab_d[:, :, t0:t0 + T].rearrange("a r t -> r a t"), (), [td.b], td.b)
                    if tb + 1 < nblk:
                        hb_next = load1(tb + 1)
                    pq0 = pps.next(); pq1 = pps.next(); pkv = pps.next(); pka = pps.next(); pkb = pps.next()
                    for k in range(8):
                        kb.mm(pq0.t[:], Win.t[:, k, 0:128], hn.t[:, k, :], k == 0, k == 7, [Win.b, hn.b], [pq0.b])
                    for k in range(8):
                        kb.mm(pq1.t[0:64, :], Win.t[:, k, 128:192], hn.t[:, k, :], k == 0, k == 7, [Win.b, hn.b], [pq1.b])
                    for k in range(8):
                        kb.mm(pkv.t[:], Win.t[:, k, 192:320], hn.t[:, k, :], k == 0, k == 7, [Win.b, hn.b], [pkv.b])
                    for k in range(8):
                        kb.mm(pka.t[0:32, :], Win.t[:, k, 320:352], hn.t[:, k, :], k == 0, k == 7, [Win.b, hn.b], [pka.b])
                    for k in range(8):
                        kb.mm(pkb.t[0:32, :], Winp.t[:, k, 0:32], hn.t[:, k, :], k == 0, k == 7, [Winp.b, hn.b], [pkb.b])
                    rs = rss.next()
                    head_rstd(pq0, 128, 192, sqs.next(), ssqs.next(), rs, extra=(pq1, 64, sqs.next()))
                    kb.stt(cqn0.t[:], pq0.t[:], sv.t[:, 0:1], rs.t[:], ALU.mult, ALU.mult, [pq0.b, sv.b, rs.b], [cqn0.b])
                    kb.stt(cqn1.t[:], pq1.t[0:64, :], sv.t[0:64, 1:2], rs.t[0:64, :], ALU.mult, ALU.mult, [pq1.b, sv.b, rs.b], [cqn1.b])
                    rs2 = rss.next()
                    head_rstd(pkv, 128, 128, sqs.next(), ssqs.next(), rs2)
                    kb.stt(ckvn.t[:], pkv.t[:], sv.t[:, 2:3], rs2.t[:], ALU.mult, ALU.mult, [pkv.b, sv.b, rs2.b], [ckvn.b])
                    ob = rope_out(pka, pkb, 32, tk.t[:, 0, :], tk.t[:, 1, :], tk.b, None, "gpsimd")
                    for h in range(8):
                        kb.dma("gpsimd", km[h, 64:96, t0:t0 + T], ob.t[0:32, :], [ob.b], (), ob.b)
                    for h in range(8):
                        A = pps.next(); B = pps.next()
                        kb.mm(A.t[0:96, :], Wuq.t[:, 0, h * 96:(h + 1) * 96], cqn0.t[:], True, False, [Wuq.b, cqn0.b], [A.b])
                        kb.mm(A.t[0:96, :], Wuq.t[0:64, 1, h * 96:(h + 1) * 96], cqn1.t[:], False, True, [Wuq.b, cqn1.b], [A.b])
                        kb.mm(B.t[0:96, :], Wuqp.t[:, 0, h * 96:(h + 1) * 96], cqn0.t[:], True, False, [Wuqp.b, cqn0.b], [B.b])
                        kb.mm(B.t[0:96, :], Wuqp.t[0:64, 1, h * 96:(h + 1) * 96], cqn1.t[:], False, True, [Wuqp.b, cqn1.b], [B.b])
                        ob = rope_out(A, B, 96, tm.t[:, 0, :], tm.t[:, 1, :], tm.b, None, "gpsimd" if h % 2 else "vector", rot=obs2)
                        kb.dma("sync", qm[h, :, t0:t0 + T], ob.t[0:96, :], [ob.b], (), ob.b)
                    for h in range(8):
                        A = pps.next(); ob = obs2.next()
                        kb.mm(A.t[0:64, :], Wukv.t[:, 0, h * 128:h * 128 + 64], ckvn.t[:], True, True, [Wukv.b, ckvn.b], [A.b])
                        kb.cp("scalar" if h % 2 else "vector", ob.t[0:64, :], A.t[0:64, :], [A.b], [ob.b])
                        kb.dma("sync", km[h, 0:64, t0:t0 + T], ob.t[0:64, :], [ob.b], (), ob.b)
                    wv = Wukv.t[:, 0, :].rearrange("p (h c) -> p h c", c=128)[:, :, 64:128]
                    vt = vts.next()
                    for tt in range(4):
                        A = pps.next()
                        kb.mm(A.t[:], ckvn.t[:, tt * 128:(tt + 1) * 128], wv, True, True, [Wukv.b, ckvn.b], [A.b])
                        kb.cp("scalar" if tt % 2 else "vector", vt.t[:, :, tt, 0:64], A.t[:].rearrange("p (h c) -> p h c", c=64), [A.b], [vt.b])
                    kb.dma("gpsimd", vmv[:, :, tb * 4:tb * 4 + 4, :], vt.t[:], [vt.b], (), vt.b)
                    if tb + 1 < nblk:
                        hn_next = hns.next()
                        norm_block(st, hb_next, hn_next, T, 0, sqs, ssqs.next(), rss.next())
                    for which, colA, colB, dst in ((0, 352, 32, qd), (1, 864, 544, kd)):
                        for h in range(4):
                            A = pps.next(); B = pps.next()
                            for k in range(8):
                                kb.mm(A.t[:], Win.t[:, k, colA + h * 128:colA + (h + 1) * 128], hn.t[:, k, :], k == 0, k == 7, [Win_b2, hn.b], [A.b])
                            abf = abfs.next()
                            kb.cp("scalar", abf.t[:], A.t[:], [A.b], [abf.b])
                            kb.mm(B.t[:], PM.t[:, 0:128], abf.t[:], True, True, [PM.b, abf.b], [B.b])
                            ob = rope_out(A, B, 128, td.t[:, 0, :], td.t[:, 1, :], td.b, None, "gpsimd", rot=obs2)
                            kb.dma("sync", dst[h, :, t0:t0 + T], ob.t[:], [ob.b], (), ob.b)
                    vdt = vds.next()
                    for tt in range(4):
                        A = pps.next()
                        for k in range(8):
                            kb.mm(A.t[:], hn.t[:, k, tt * 128:(tt + 1) * 128], Win.t[:, k, 1376:1888], k == 0, k == 7, [Win_b2, hn.b], [A.b])
                        kb.cp("scalar" if tt % 2 else "vector", vdt.t[:, :, tt, :], A.t[:].rearrange("p (h c) -> p h c", c=128), [A.b], [vdt.b])
                    kb.dma("gpsimd", vdv[:, :, tb * 4:tb * 4 + 4, :], vdt.t[:], [vdt.b], (), vdt.b)
                S.emit()
        if stop_after <= 1:
            return nc

        class Job:
            pass

        def run_jobs(jobs, hooks):
            if not jobs:
                return
            jobs[0].qk(0)
            jobs[0].qk(1)
            for ji, job in enumerate(jobs):
                nxt = jobs[ji + 1] if ji + 1 < len(jobs) else None
                if job.on_start is not None:
                    job.on_start()
                for pk in range(job.n):
                    if pk + 2 < job.n:
                        job.qk(pk + 2)
                    elif nxt is not None:
                        nxt.qk(pk + 2 - job.n)
                    job.pv(pk)
                    for fn in hooks.pop(pk, []):
                        fn()
                job.epilogue()

        def core_job(Kt_ap_fn, Q_ap, Vt_fn, V_b, KQ_reads, scale, Ss, Ps, O, L, epilogue, on_start=None):
            job = Job()
            job.n = 16
            job.on_start = on_start
            job.epilogue = epilogue
            Pq = {}

            def qk(kp):
                Sx = Ss.next()
                for j in range(2):
                    kc = kp * 2 + j
                    kb.mm(Sx.t[:, j * 512:(j + 1) * 512], Kt_ap_fn(kc), Q_ap, True, True, KQ_reads, [Sx.b])
                P = Ps.next()
                kb.act(P.t[:], Sx.t[:], AF.Exp, [Sx.b], [P.b], scale=scale)
                Pq[kp] = P

            def pv(pk):
                PP = Pq[pk]
                for j in range(2):
                    kc = pk * 2 + j
                    kb.mm(O.t[:], Vt_fn(kc), PP.t[:, j * 512:(j + 1) * 512], kc == 0, kc == 31, [V_b, PP.b], [O.b])
                if L is not None and pk % 2 == 1:
                    prevP = Pq[pk - 1]
                    for j, (Pt, hf) in enumerate(((prevP, 0), (prevP, 1), (PP, 0), (PP, 1))):
                        kb.mm(L.t[32 * j:32 * j + 32, :], ones.t[:, 0:32], Pt.t[:, hf * 512:(hf + 1) * 512], pk == 1, pk == job.n - 1,
                              [ones.b, Pt.b], [L.b], tp=(0, 32 * j))
            job.qk = qk
            job.pv = pv
            return job

        def combine_L(L, lc, p0, p1):
            kb.cp("vector", lc.t[p0:p1, :], L.t[p0:p1, :], [L.b], [lc.b])
            kb.mm(L.t[:], sel.t[p0:p1, :], lc.t[p0:p1, :], True, True, [sel.b, lc.b], [L.b])

        def diff_job(Kt, Qt, q0, Vt, scale, Ss, Ps, O0, O1, L01, epilogue, on_start=None):
            job = Job()
            job.n = 32
            job.on_start = on_start
            job.epilogue = epilogue
            Pq = {}

            def qk(kp):
                Sx = Ss.next()
                for c in range(2):
                    kb.mm(Sx.t[:, c * 512:(c + 1) * 512], Kt.t[64 * c:64 * c + 64, kp * 128:(kp + 1) * 128],
                          Qt.t[64 * c:64 * c + 64, q0:q0 + 512], True, True, [Kt.b, Qt.b], [Sx.b])
                P = Ps.next()
                kb.act(P.t[:], Sx.t[:], AF.Exp, [Sx.b], [P.b], scale=scale)
                Pq[kp] = P

            def pv(pk):
                PP = Pq[pk]
                kb.mm(O0.t[:], Vt.t[:, pk, :], PP.t[:, 0:512], pk == 0, pk == job.n - 1, [Vt.b, PP.b], [O0.b])
                kb.mm(O1.t[:], Vt.t[:, pk, :], PP.t[:, 512:1024], pk == 0, pk == job.n - 1, [Vt.b, PP.b], [O1.b])
                if pk % 2 == 1:
                    prevP = Pq[pk - 1]
                    for j, (Pt, c) in enumerate(((prevP, 0), (PP, 0), (prevP, 1), (PP, 1))):
                        kb.mm(L01.t[32 * j:32 * j + 32, :], ones.t[:, 0:32], Pt.t[:, c * 512:(c + 1) * 512], pk == 1, pk == job.n - 1,
                              [ones.b, Pt.b], [L01.b], tp=(0, 32 * j))
            job.qk = qk
            job.pv = pv
            return job

        stF = gst.enter_context(ExitStack())
        Wg0 = kb.sb(stF, "Wg0", [128, 8, FF], BF16)
        Wu0 = kb.sb(stF, "Wu0", [128, 8, FF], BF16)
        Wo0 = kb.sb(stF, "Wo0", [128, 8, D], BF16)
        stgF = None
        pre = kb.load_w_thunks(stgF, e_w_out, Wo0, Wo0.b, 8, 0, D, CH=1024) + \
            kb.load_w_thunks(stgF, w_gate[0], Wg0, Wg0.b, 8, 0, FF, CH=1408) + \
            kb.load_w_thunks(stgF, w_up[0], Wu0, Wu0.b, 8, 0, FF, CH=1408)
        with ExitStack() as st:
          if 2 not in skip:
            Qts = Rot([kb.sb(st, "Qt", [128, SEQ], BF16) for _ in range(2)])
            Kts = Rot([kb.sb(st, "Kt", [128, SEQ], BF16) for _ in range(2)])
            Vts = Rot([kb.sb(st, "Vt", [128, 32, 128], BF16) for _ in range(2)])
            Ps = Rot([kb.sb(st, "P", [128, 1024], BF16) for _ in range(6)])
            rcs = Rot([kb.sb(st, "rc", [128, 512], F32) for _ in range(2)])
            lcs = Rot([kb.sb(st, "lc", [128, 512], F32) for _ in range(2)])
            tcs = Rot([kb.sb(st, "tc", [128, 512], F32) for _ in range(4)])
            ofs = Rot([kb.sb(st, "of", [128, 512], F32) for _ in range(2)])
            sqs = Rot([kb.sb(st, "sq", [128, 512], BF16) for _ in range(2)])
            rss = Rot([kb.sb(st, "rs", [128, 512], F32) for _ in range(2)])
            obs = Rot([kb.sb(st, "ob", [128, 512], BF16) for _ in range(3)])
            lv = kb.sb(st, "lv", [128, 4, 64], F32)
            ltmp = kb.sb(st, "ltmp", [128, 2, 64], F32)
            lsum = kb.sb(st, "lsum", [128, 2], F32)
            Ss = Rot([kb.ps(st, "S", [128, 1024]) for _ in range(2)])
            Os = Rot([kb.ps(st, "O", [128, 512]) for _ in range(2)])
            L01 = kb.ps(st, "L01", [128, 512])
            X = kb.ps(st, "X", [128, 512])
            oss = Rot([kb.sb(st, "os", [128, 512], F32) for _ in range(4)])
            hooks = {}
            lam_init = 0.8 - 0.6 * math.exp(-0.3 * 0)
            kb.dma("sync", lv.t[:], lamv, (), [lv.b], lv.b)
            kb.tt("vector", ltmp.t[:, 0, :], lv.t[:, 0, :], lv.t[:, 1, :], ALU.mult, [lv.b], [ltmp.b])
            kb.tt("vector", ltmp.t[:, 1, :], lv.t[:, 2, :], lv.t[:, 3, :], ALU.mult, [lv.b], [ltmp.b])
            S.op("vector", lambda e: e.reduce_sum(out=lsum.t[:, 0:1], in_=ltmp.t[:, 0, :], axis=AX.X), [ltmp.b], [lsum.b])
            S.op("vector", lambda e: e.reduce_sum(out=lsum.t[:, 1:2], in_=ltmp.t[:, 1, :], axis=AX.X), [ltmp.b], [lsum.b])
            kb.act(lsum.t[:], lsum.t[:], AF.Exp, [lsum.b], [lsum.b])
            kb.tt("vector", lam_t.t[:, 0:1], lsum.t[:, 1:2], lsum.t[:, 0:1], ALU.subtract, [lsum.b], [lam_t.b])
            kb.ts("vector", lam_t.t[:, 0:1], lam_t.t[:, 0:1], -lam_init, ALU.add, [lam_t.b], [lam_t.b])
            kb.ts("vector", lam_t.t[:, 1:2], sv.t[:, 3:4], 1.0 - lam_init, ALU.mult, [sv.b], [lam_t.b])


            def load_head_tiles(idx):
                return Qts.next(), Kts.next(), Vts.next()

            def load_head(idx, Qt, Kt, Vt):
                if idx < 8:
                    h = idx
                    kb.dma("sync", Qt.t[0:96, :], qm[h], (), [Qt.b], Qt.b)
                    kb.dma("sync", Kt.t[0:96, :], km[h], (), [Kt.b], Kt.b)
                    kb.dma("sync", Vt.t[:], vm[h].rearrange("p (j c) -> p j c", c=128), (), [Vt.b], Vt.b)
                else:
                    h = idx - 8
                    kb.dma("sync", Qt.t[:], qd[h], (), [Qt.b], Qt.b)
                    kb.dma("sync", Kt.t[:], kd[h], (), [Kt.b], Kt.b)
                    kb.dma("sync", Vt.t[:], vd[h].rearrange("p (j c) -> p j c", c=128), (), [Vt.b], Vt.b)

            heads = [load_head_tiles(i) for i in range(12)]
            jobs = []
            for idx in range(12):
                Qt, Kt, Vt = heads[idx]
                for qb in range(8):
                    q0 = qb * 512

                    def on_start(idx=idx, qb=qb):
                        if qb == 0:
                            if idx == 0:
                                pass
                            if idx + 1 < 12:
                                load_head(idx + 1, *heads[idx + 1])
                        if pre and idx >= 1:
                            pre.pop(0)()

                    if idx < 8:
                        h = idx
                        O = Os.next()

                        def epi_mla(O=O, h=h, q0=q0):
                            rc = rcs.next(); ob = obs.next()
                            kb.recip(rc.t[64:128, :], O.t[64:128, :], [O.b], [rc.b])
                            kb.tt("vector", ob.t[0:64, :], O.t[0:64, :], rc.t[64:128, :], ALU.mult, [O.b, rc.b], [ob.b])
                            kb.dma("gpsimd", aT[h * 64:(h + 1) * 64, q0:q0 + 512], ob.t[0:64, :], [ob.b], (), ob.b)

                        jobs.append(core_job(lambda kc, Kt=Kt: Kt.t[0:96, kc * 128:(kc + 1) * 128], Qt.t[0:96, q0:q0 + 512],
                                             lambda kc, Vt=Vt: Vt.t[:, kc, :], Vt.b, [Kt.b, Qt.b], 96 ** -0.5, Ss, Ps, O, None, epi_mla, on_start))
                    else:
                        h = idx - 8
                        O0 = Os.next(); O1 = Os.next()

                        def epi_diff(O0=O0, O1=O1, h=h, q0=q0):
                            o0s = oss.next(); o1s = oss.next(); lc = lcs.next()
                            kb.cp("vector", o0s.t[:], O0.t[:], [O0.b], [o0s.b])
                            kb.cp("vector", o1s.t[:], O1.t[:], [O1.b], [o1s.b])
                            kb.cp("vector", lc.t[:], L01.t[:], [L01.b], [lc.b])
                            box = []

                            def stage_bc(c, osb):
                                rc = rcs.next(); tc = tcs.next()
                                kb.mm(X.t[:], sel.t[64 * c:64 * c + 64, :], lc.t[64 * c:64 * c + 64, :], True, True, [sel.b, lc.b], [X.b])
                                kb.recip(rc.t[:], X.t[:], [X.b], [rc.b])
                                kb.tt("vector", tc.t[:], osb.t[:], rc.t[:], ALU.mult, [osb.b, rc.b], [tc.b])
                                box.append(tc)

                            def stage_d():
                                of = ofs.next(); sq = sqs.next(); rs = rss.next(); ob = obs.next()
                                kb.stt(of.t[:], box[1].t[:], lam_t.t[:, 0:1], box[0].t[:], ALU.mult, ALU.add, [box[0].b, box[1].b, lam_t.b], [of.b])
                                kb.tt("gpsimd", sq.t[:], of.t[:], of.t[:], ALU.mult, [of.b], [sq.b])
                                kb.mm(X.t[:], ones.t[:], sq.t[:], True, True, [ones.b, sq.b], [X.b])
                                kb.act(rs.t[:], X.t[:], AF.Ln, [X.b], [rs.b], scale=1.0 / 128, bias=EPS)
                                kb.act(rs.t[:], rs.t[:], AF.Exp, [rs.b], [rs.b], scale=-0.5)
                                kb.stt(ob.t[:], of.t[:], lam_t.t[:, 1:2], rs.t[:], ALU.mult, ALU.mult, [of.b, lam_t.b, rs.b], [ob.b])
                                kb.dma("gpsimd", aT[512 + h * 128:512 + (h + 1) * 128, q0:q0 + 512], ob.t[:], [ob.b], (), ob.b)

                            hooks.setdefault(2, []).append(lambda: stage_bc(0, o0s))
                            hooks.setdefault(8, []).append(lambda: stage_bc(1, o1s))
                            hooks.setdefault(16, []).append(stage_d)

                        jobs.append(diff_job(Kt, Qt, q0, Vt, 64 ** -0.5, Ss, Ps, O0, O1, L01, epi_diff, on_start))
            load_head(0, *heads[0])
            run_jobs(jobs, hooks)
            for kp in sorted(hooks):
                for fn in hooks[kp]:
                    fn()
            hooks.clear()
            while pre:
                pre.pop(0)()
            S.emit()
        if stop_after <= 2:
            return nc

        def outproj_phase(Wo, a_dram, hin, hout, extra):
            with ExitStack() as st:
                T = 256
                for t in extra:
                    t()
                hbs = Rot([kb.sb(st, "hb", [128, 8, T], F32, 8) for _ in range(2)])
                ats = Rot([kb.sb(st, "at", [128, 8, T], BF16) for _ in range(2)])
                pps = Rot([kb.ps(st, "pp", [128, 512]) for _ in range(4)])
                hv = cv(hin); av = cv(a_dram); ov = cv(hout)
                def load(tb):
                    hb = hbs.next(); at = ats.next()
                    kb.dma("sync", at.t[:], av[:, :, tb * T:(tb + 1) * T], (), [at.b], at.b)
                    kb.dma("sync", hb.t[:], hv[:, :, tb * T:(tb + 1) * T], (), [hb.b] + hb.cb, hb.b)
                    return hb, at

                nxt = load(0)
                for tb in range(SEQ // T):
                    t0 = tb * T
                    hb, at = nxt
                    if tb + 1 < SEQ // T:
                        nxt = load(tb + 1)
                    for o in range(8):
                        pp = pps.next()
                        for k in range(8):
                            kb.mm(pp.t[:, 0:T], Wo.t[:, k, o * 128:(o + 1) * 128], at.t[:, k, :], k == 0, k == 7, [Wo.b, at.b], [pp.b])
                        kb.tt("vector", hb.t[:, o, :], hb.t[:, o, :], pp.t[:, 0:T], ALU.add, [hb.b, pp.b], [hb.cb[o]])
                    kb.dma("sync", ov[:, :, t0:t0 + T], hb.t[:], [hb.b] + hb.cb, (), hb.b)
                S.emit()

        while pre:
            pre.pop(0)()
        Wd0 = kb.sb(stF, "Wd0", [128, NM, D], BF16)
        wd_thunks = kb.load_w_thunks(stgF, w_down[0], Wd0, Wd0.b, NM, 0, D, CH=1024)
        if stop_after <= 3:
            return nc

        def ffn_phase(layer, gcol, hin, hout, final, weights=None, w16=None, attn=None, extra=()):
            with ExitStack() as st:
                T = 256
                GRP = ((0, 6), (6, 12), (12, 17), (17, 22))
                grp_of = [gi for gi, (g0, g1) in enumerate(GRP) for _ in range(g0, g1)]
                if weights is not None:
                    Wg, Wu, Wd = weights
                    wgb = [Wg.b] * 4
                    wub = [Wu.b] * 4
                else:
                  wgb = [Buf("wg%d_%d" % (layer, i)) for i in range(4)]
                  wub = [Buf("wu%d_%d" % (layer, i)) for i in range(4)]
                  Wg = kb.sb(st, "Wg", [128, 8, FF], BF16)
                  Wu = kb.sb(st, "Wu", [128, 8, FF], BF16)
                  Wd = kb.sb(st, "Wd", [128, NM, D], BF16)
                  if w16 is None:
                      for gi, (g0, g1) in enumerate(GRP):
                          kb.load_w(None, w_gate[layer], Wg, wgb[gi], 8, g0 * 128, g1 * 128, dst_c0=g0 * 128)
                          kb.load_w(None, w_up[layer], Wu, wub[gi], 8, g0 * 128, g1 * 128, dst_c0=g0 * 128)
                      kb.load_w(None, w_down[layer], Wd, Wd.b, NM, 0, D, CH=1024)
                  else:
                      g16, u16, d16 = w16
                      g16v = g16.rearrange("(k p) c -> p k c", p=128)
                      u16v = u16.rearrange("(k p) c -> p k c", p=128)
                      d16v = d16.rearrange("(m p) c -> p m c", p=128)
                      for gi, (g0, g1) in enumerate(GRP):
                          kb.dma("scalar", Wg.t[:, :, g0 * 128:g1 * 128], g16v[:, :, g0 * 128:g1 * 128], (), [wgb[gi]], wgb[gi])
                          kb.dma("scalar", Wu.t[:, :, g0 * 128:g1 * 128], u16v[:, :, g0 * 128:g1 * 128], (), [wub[gi]], wub[gi])
                      for m0 in range(0, NM, 6):
                          m1 = min(NM, m0 + 6)
                          kb.dma("scalar", Wd.t[:, m0:m1, :], d16v[:, m0:m1, :], (), [Wd.b], Wd.b, fill=True)
                hbs = Rot([kb.sb(st, "hb", [128, 8, T], F32, 8) for _ in range(2)])
                hns = Rot([kb.sb(st, "hn", [128, 8, T], BF16) for _ in range(2)])
                acs = Rot([kb.sb(st, "ac", [128, NM, T], BF16, NM) for _ in range(1 if attn else 2)])
                ats = Rot([kb.sb(st, "at", [128, 8, T], BF16) for _ in range(2)]) if attn else None
                for t in extra:
                    t()
                sqs = Rot([kb.sb(st, "sq", [128, T], BF16) for _ in range(3)])
                sgs = Rot([kb.sb(st, "sg", [128, T], F32) for _ in range(3)])
                rss = Rot([kb.sb(st, "rs", [128, T], F32) for _ in range(2)])
                ofs = Rot([kb.sb(st, "of", [128, 8, T], F32) for _ in range(1)]) if final else None
                pgs = Rot([kb.ps(st, "pg", [128, 512]) for _ in range(4)])
                pos = Rot([kb.ps(st, "po", [128, 512]) for _ in range(2)])
                ssqs = Rot([kb.ps(st, "ssq", [128, 512]) for _ in range(2)])
                hv = cv(hin); ov = cv(hout)
                nblk = SEQ // T

                def load(tb):
                    hb = hbs.next()
                    if attn:
                        at = ats.next()
                        kb.dma("sync", at.t[:], cv(attn[1])[:, :, tb * T:(tb + 1) * T], (), [at.b], at.b)
                        hb.at = at
                    kb.dma("sync", hb.t[:], hv[:, :, tb * T:(tb + 1) * T], (), [hb.b] + hb.cb, hb.b)
                    return hb

                def pre_norm(hb):
                    if not attn:
                        return
                    Wo_, at = attn[0], hb.at
                    for o in range(8):
                        pp = pos.next()
                        for k in range(8):
                            kb.mm(pp.t[:, 0:T], Wo_.t[:, k, o * 128:(o + 1) * 128], at.t[:, k, :], k == 0, k == 7, [Wo_.b, at.b], [pp.b])
                        kb.tt("vector", hb.t[:, o, :], hb.t[:, o, :], pp.t[:, 0:T], ALU.add, [hb.b, pp.b], [hb.cb[o]])

                nxt = load(0)
                hn_next = hns.next()
                pre_norm(nxt)
                norm_block(st, nxt, hn_next, T, gcol, sqs, ssqs.next(), rss.next())
                for tb in range(nblk):
                    t0 = tb * T
                    hb = nxt
                    hn = hn_next
                    ac = acs.next()
                    if tb + 1 < nblk:
                        nxt = load(tb + 1)
                    for m in range(NM):
                        pg = pgs.next()
                        for k in range(8):
                            kb.mm(pg.t[:, 0:T], Wg.t[:, k, m * 128:(m + 1) * 128], hn.t[:, k, :], k == 0, k == 7, [wgb[grp_of[m]], hn.b], [pg.b])
                        for k in range(8):
                            kb.mm(pg.t[:, T:2 * T], Wu.t[:, k, m * 128:(m + 1) * 128], hn.t[:, k, :], k == 0, k == 7, [wub[grp_of[m]], hn.b], [pg.b])
                        sg = sgs.next()
                        kb.act(sg.t[:], pg.t[:, 0:T], AF.Silu, [pg.b], [sg.b])
                        kb.tt("vector", ac.t[:, m, :], sg.t[:], pg.t[:, T:2 * T], ALU.mult, [sg.b, pg.b], [ac.cb[m]])
                    if tb + 1 < nblk:
                        hn_next = hns.next()
                        pre_norm(nxt)
                        norm_block(st, nxt, hn_next, T, gcol, sqs, ssqs.next(), rss.next())
                    for o in range(8):
                        po = pos.next()
                        for m in range(NM):
                            kb.mm(po.t[:, 0:T], Wd.t[:, m, o * 128:(o + 1) * 128], ac.t[:, m, :], m == 0, m == NM - 1, [Wd.b, ac.cb[m]], [po.b])
                        kb.tt("vector", hb.t[:, o, :], hb.t[:, o, :], po.t[:, 0:T], ALU.add, [hb.b, po.b], [hb.cb[o]])
                    if not final:
                        kb.dma("sync", ov[:, :, t0:t0 + T], hb.t[:], [hb.b] + hb.cb, (), hb.b)
                    else:
                        of = ofs.next()
                        ssq = ssqs.next(); rs = rss.next()
                        for c in range(8):
                            sq = sqs.next()
                            kb.act(sq.t[:, 0:T], hb.t[:, c, :], AF.Square, [hb.cb[c]], [sq.b])
                            kb.mm(ssq.t[:, 0:T], ones.t[:], sq.t[:, 0:T], c == 0, c == 7, [ones.b, sq.b], [ssq.b])
                        kb.act(rs.t[:, 0:T], ssq.t[:, 0:T], AF.Ln, [ssq.b], [rs.b], scale=1.0 / D, bias=EPS)
                        kb.act(rs.t[:, 0:T], rs.t[:, 0:T], AF.Exp, [rs.b], [rs.b], scale=-0.5)
                        for c in range(8):
                            kb.stt(of.t[:, c, :], hb.t[:, c, :], gn.t[:, 32 + c:33 + c], rs.t[:, 0:T], ALU.mult, ALU.mult,
                                   [hb.cb[c], gn.b, rs.b], [of.b])
                        kb.dma("sync", ov[:, :, t0:t0 + T], of.t[:], [of.b], (), of.b)
                S.emit()

        if 4 not in skip:
            ffn_phase(0, 8, xT, h2T, False, weights=(Wg0, Wu0, Wd0), attn=(Wo0, aT), extra=wd_thunks)
        stF.close()
        if stop_after <= 4:
            return nc

        with ExitStack() as st:
            T = 512
            Kt = [kb.sb(st, "Kt", [128, SEQ], BF16) for _ in range(2)]
            Vt = [kb.sb(st, "Vt", [128, 32, 128], BF16) for _ in range(2)]
            h2v = cv(h2T)
            Wq = kb.sb(st, "Wq", [128, 8, D], BF16)
            Wo = kb.sb(st, "Wo2", [128, 8, D], BF16)
            stg6 = None
            pre6 = kb.load_w_thunks(stg6, o_w_qkv, Wq, Wq.b, 8, 0, 1024, CH=1024) + \
                kb.load_w_thunks(stg6, o_w_out, Wo, Wo.b, 8, 0, D, CH=1024)
            with ExitStack() as st5:
                Wk = kb.sb(st5, "Wk", [128, 8, 256], BF16)
                abfs = Rot([kb.sb(st5, "abf", [128, T], BF16) for _ in range(2)])
                Wv = kb.sb(st5, "Wv", [128, 8, 256], BF16)
                kb.load_w(None, o_w_qkv, Wk, Wk.b, 8, 1024, 1280)
                kb.load_w(None, o_w_qkv, Wv, Wv.b, 8, 1280, 1536)
                hbs = Rot([kb.sb(st5, "hb", [128, 8, T], F32) for _ in range(2)])
                hns = Rot([kb.sb(st5, "hn", [128, 8, T], BF16) for _ in range(2)])
                tgs = Rot([kb.sb(st5, "tg", [128, 2, T], F32) for _ in range(2)])
                sqs = Rot([kb.sb(st5, "sq", [128, T], BF16) for _ in range(3)])
                rss = Rot([kb.sb(st5, "rs", [128, T], F32) for _ in range(2)])
                t1s = Rot([kb.sb(st5, "t1", [128, T], F32) for _ in range(2)])
                t2s = Rot([kb.sb(st5, "t2", [128, T], F32) for _ in range(2)])
                t3s = Rot([kb.sb(st5, "t3", [128, T], F32) for _ in range(2)])
                pps = Rot([kb.ps(st5, "pp", [128, 512]) for _ in range(6)])
                ssqs = Rot([kb.ps(st5, "ssq", [128, 512]) for _ in range(2)])
                def load5(tb):
                    hb = hbs.next()
                    kb.dma("sync", hb.t[:], h2v[:, :, tb * T:(tb + 1) * T], (), [hb.b], hb.b)
                    return hb

                hb_next = load5(0)
                hn_next = hns.next()
                norm_block(st5, hb_next, hn_next, T, 16, sqs, ssqs.next(), rss.next())
                for tb in range(SEQ // T):
                    t0 = tb * T
                    hn = hn_next
                    tg = tgs.next()
                    kb.dma("sync", tg.t[:], tab_g[:, :, t0:t0 + T].rearrange("a r t -> r a t"), (), [tg.b], tg.b)
                    if tb + 1 < SEQ // T:
                        hb_next = load5(tb + 1)
                    for _ in range(3):
                        if pre6:
                            pre6.pop(0)()
                    for kvh in range(2):
                        A = pps.next(); B = pps.next()
                        for k in range(8):
                            kb.mm(A.t[:], Wk.t[:, k, kvh * 128:(kvh + 1) * 128], hn.t[:, k, :], k == 0, k == 7, [Wk.b, hn.b], [A.b])
                        abf = abfs.next()
                        kb.cp("scalar", abf.t[:], A.t[:], [A.b], [abf.b])
                        kb.mm(B.t[:], PM.t[:, 128:256], abf.t[:], True, True, [PM.b, abf.b], [B.b])
                        rs = rss.next()
                        head_rstd(A, 128, 128, sqs.next(), ssqs.next(), rs)
                        t1 = t1s.next(); t2 = t2s.next()
                        kb.stt(t1.t[:], A.t[:], sv.t[:, 6:7], tg.t[:, 0, :], ALU.mult, ALU.mult, [A.b, sv.b, tg.b], [t1.b])
                        kb.stt(t2.t[:], B.t[:], sv.t[:, 7:8], tg.t[:, 1, :], ALU.mult, ALU.mult, [B.b, sv.b, tg.b], [t2.b])
                        t3 = t3s.next()
                        kb.tt("gpsimd", t3.t[:], t1.t[:], t2.t[:], ALU.add, [t1.b, t2.b], [t3.b])
                        kb.tt("vector", Kt[kvh].t[:, t0:t0 + T], t3.t[:], rs.t[:], ALU.mult, [t3.b, rs.b], [Kt[kvh].b])
                    if tb + 1 < SEQ // T:
                        hn_next = hns.next()
                        norm_block(st5, hb_next, hn_next, T, 16, sqs, ssqs.next(), rss.next())
                    for tt in range(4):
                        A = pps.next()
                        for k in range(8):
                            kb.mm(A.t[:, 0:256], hn.t[:, k, tt * 128:(tt + 1) * 128], Wv.t[:, k, :], k == 0, k == 7, [Wv.b, hn.b], [A.b])
                        for kvh in range(2):
                            kb.cp("scalar" if kvh else "vector", Vt[kvh].t[:, tb * 4 + tt, :], A.t[:, kvh * 128:(kvh + 1) * 128], [A.b], [Vt[kvh].b])
                while pre6:
                    pre6.pop(0)()
                S.emit()
            if stop_after <= 5:
                return nc
            pcb = Buf("precast")
            pre16 = []
            for k in range(8):
                for hf in range(2):
                    for src, dst in ((w_gate[1], wg16), (w_up[1], wu16)):
                        pre16.append(lambda k=k, hf=hf, src=src, dst=dst: kb.dma(
                            "gpsimd", dst[k * 128:(k + 1) * 128, hf * 1408:(hf + 1) * 1408], src[k * 128:(k + 1) * 128, hf * 1408:(hf + 1) * 1408],
                            (), [pcb], pcb, fill=True))
            for m in range(NM):
                pre16.append(lambda m=m: kb.dma("gpsimd", wd16[m * 128:(m + 1) * 128, :], w_down[1][m * 128:(m + 1) * 128, :], (), [pcb], pcb, fill=True))
            with ExitStack() as st6:
                hbs = Rot([kb.sb(st6, "hb", [128, 8, T], F32, 8) for _ in range(2)])
                hns = Rot([kb.sb(st6, "hn", [128, 8, T], BF16) for _ in range(2)])
                tgs = Rot([kb.sb(st6, "tg", [128, 2, T], F32) for _ in range(2)])
                Qts = Rot([kb.sb(st6, "Qt", [128, 8, T], BF16, 8) for _ in range(1)])
                ats = Rot([kb.sb(st6, "at", [128, 8, T], BF16, 8) for _ in range(1)])
                sqs = Rot([kb.sb(st6, "sq", [128, T], BF16) for _ in range(3)])
                rss = Rot([kb.sb(st6, "rs", [128, T], F32) for _ in range(2)])
                t1s = Rot([kb.sb(st6, "t1", [128, T], F32) for _ in range(2)])
                t2s = Rot([kb.sb(st6, "t2", [128, T], F32) for _ in range(2)])
                t3s = Rot([kb.sb(st6, "t3", [128, T], F32) for _ in range(2)])
                rcs = Rot([kb.sb(st6, "rc", [128, T], F32) for _ in range(2)])
                lcs = Rot([kb.sb(st6, "lc", [128, T], F32) for _ in range(2)])
                Ps = Rot([kb.sb(st6, "P", [128, 1024], BF16) for _ in range(6)])
                Ss = Rot([kb.ps(st6, "S", [128, 1024]) for _ in range(2)])
                Os = Rot([kb.ps(st6, "O", [128, 512]) for _ in range(2)])
                Ls = Rot([kb.ps(st6, "L", [128, 512]) for _ in range(2)])
                h3v = cv(h3T)
                hooks6 = {}
                abfs6 = Rot([kb.sb(st6, "abf", [128, T], BF16) for _ in range(2)])
                def load6(qb):
                    hb = hbs.next(); tg = tgs.next()
                    kb.dma("sync", hb.t[:], h2v[:, :, qb * T:(qb + 1) * T], (), [hb.b] + hb.cb, hb.b)
                    kb.dma("sync", tg.t[:], tab_g[:, :, qb * T:(qb + 1) * T].rearrange("a r t -> r a t"), (), [tg.b], tg.b)
                    return hb, tg

                nxt6 = load6(0)
                hn_next6 = [hns.next()]
                norm_block(st6, nxt6[0], hn_next6[0], T, 16, sqs, Ls.next(), rss.next())
                for qb in range(SEQ // T):
                    t0 = qb * T
                    hb, tg = nxt6
                    hn = hn_next6[0]
                    Qt = Qts.next(); at = ats.next()
                    if qb + 1 < SEQ // T:
                        nxt6 = load6(qb + 1)
                    for hq in range(8):
                        A = Os.next(); B = Os.next()
                        for k in range(8):
                            kb.mm(A.t[:], Wq.t[:, k, hq * 128:(hq + 1) * 128], hn.t[:, k, :], k == 0, k == 7, [Wq.b, hn.b], [A.b])
                        abf = abfs6.next()
                        kb.cp("scalar", abf.t[:], A.t[:], [A.b], [abf.b])
                        kb.mm(B.t[:], PM.t[:, 128:256], abf.t[:], True, True, [PM.b, abf.b], [B.b])
                        rs = rss.next()
                        head_rstd(A, 128, 128, sqs.next(), Ls.next(), rs)
                        t1 = t1s.next(); t2 = t2s.next()
                        kb.stt(t1.t[:], A.t[:], sv.t[:, 4:5], tg.t[:, 0, :], ALU.mult, ALU.mult, [A.b, sv.b, tg.b], [t1.b])
                        kb.stt(t2.t[:], B.t[:], sv.t[:, 5:6], tg.t[:, 1, :], ALU.mult, ALU.mult, [B.b, sv.b, tg.b], [t2.b])
                        t3 = t3s.next()
                        kb.tt("gpsimd", t3.t[:], t1.t[:], t2.t[:], ALU.add, [t1.b, t2.b], [t3.b])
                        kb.tt("vector", Qt.t[:, hq, :], t3.t[:], rs.t[:], ALU.mult, [t3.b, rs.b], [Qt.cb[hq]])
                    jobs6 = []
                    for hq in range(8):
                        kvh = hq // 4
                        O = Os.next(); L = Ls.next()

                        def epi6(O=O, L=L, at=at, hq=hq):
                            lc = lcs.next()
                            kb.cp("vector", lc.t[:], L.t[:], [L.b], [lc.b])

                            def epi():
                                rc = rcs.next()
                                kb.mm(L.t[:], sel.t[:], lc.t[:], True, True, [sel.b, lc.b], [L.b])
                                kb.recip(rc.t[:], L.t[:], [L.b], [rc.b])
                                kb.tt("vector", at.t[:, hq, :], O.t[:], rc.t[:], ALU.mult, [O.b, rc.b], [at.cb[hq]])
                            hooks6.setdefault(1, []).append(epi)

                        def on6(hq=hq, qb=qb, nb=nxt6):
                            if pre16:
                                pre16.pop(0)()
                            if hq == 5 and qb + 1 < SEQ // T:
                                hn_next6[0] = hns.next()
                                norm_block(st6, nb[0], hn_next6[0], T, 16, sqs, Ls.next(), rss.next())

                        jobs6.append(core_job(lambda kc, kvh=kvh: Kt[kvh].t[:, kc * 128:(kc + 1) * 128], Qt.t[:, hq, :],
                                              lambda kc, kvh=kvh: Vt[kvh].t[:, kc, :], Vt[kvh].b, [Kt[kvh].b, Qt.cb[hq]], 128 ** -0.5,
                                              Ss, Ps, O, L, epi6, on6))
                    run_jobs(jobs6, hooks6)
                    for kp_ in sorted(hooks6):
                        for fn in hooks6[kp_]:
                            fn()
                    hooks6.clear()
                    for o in range(8):
                        pp = Os.next()
                        for k in range(8):
                            kb.mm(pp.t[:], Wo.t[:, k, o * 128:(o + 1) * 128], at.t[:, k, :], k == 0, k == 7, [Wo.b, at.cb[k]], [pp.b])
                        kb.tt("vector", hb.t[:, o, :], hb.t[:, o, :], pp.t[:], ALU.add, [hb.b, pp.b], [hb.cb[o]])
                    kb.dma("sync", h3v[:, :, t0:t0 + T], hb.t[:], [hb.b] + hb.cb, (), hb.b)
                while pre16:
                    pre16.pop(0)()
                S.emit()
        if stop_after <= 6:
            return nc

        ffn_phase(1, 24, h3T, outT, True, w16=(wg16, wu16, wd16))
    return nc


def _rope_tab(pos, dim, theta):
    inv = (np.float32(theta) ** (-np.arange(0, dim, 2, dtype=np.float32) / np.float32(dim))).astype(np.float32)
    ang = (pos.astype(np.float32)[:, None] * inv[None, :]).astype(np.float32)
    c = np.cos(ang.astype(np.float64)).astype(np.float32)
    s = np.sin(ang.astype(np.float64)).astype(np.float32)
    return np.concatenate([c, c], 1).T.copy(), np.concatenate([-s, s], 1).T.copy()


def _half_perm(n):
    h = n // 2
    return np.concatenate([np.arange(h, n), np.arange(0, h)])


def _tables():
    pos = np.arange(SEQ)
    c32, s32 = _rope_tab(pos, 32, 500000.0)
    tab_m = np.zeros((2, 96, SEQ), np.float32)
    tab_m[0, :64] = 1.0
    tab_m[0, 64:] = c32
    tab_m[1, 64:] = s32
    tab_kr = np.stack([c32, s32]).astype(np.float32)
    c16, s16 = _rope_tab(pos, 16, 500000.0)
    tab_d = np.zeros((2, 128, SEQ), np.float32)
    tab_d[0] = 1.0
    for g in range(2):
        tab_d[0, g * 64:g * 64 + 16] = c16
        tab_d[1, g * 64:g * 64 + 16] = s16
    cr, sr = _rope_tab(pos // 64, 64, 10000.0)
    cc, sc = _rope_tab(pos % 64, 64, 10000.0)
    tab_g = np.stack([np.concatenate([cr, cc], 0), np.concatenate([sr, sc], 0)]).astype(np.float32)
    return tab_m, tab_kr, tab_d, tab_g


def _col128(v):
    return np.ascontiguousarray(np.asarray(v, np.float32).reshape(-1, 128).T)


def _pms(dp, gp):
    m = np.zeros((128, 256), np.float32)
    m[dp, np.arange(128)] = 1.0
    m[gp, 128 + np.arange(128)] = 1.0
    return m


def _sel():
    s = np.zeros((128, 128), np.float32)
    s[0::32, :] = 1.0
    return s


_CACHE = {}


def kernel(x, e_attn_norm, e_w_in, e_q_norm, e_w_uq, e_kv_norm, e_w_ukv,
           e_lambda_q1, e_lambda_k1, e_lambda_q2, e_lambda_k2, e_subln, e_w_out,
           o_attn_norm, o_w_qkv, o_q_norm, o_k_norm, o_w_out,
           ffn_norm, w_gate, w_up, w_down, final_norm, _stop_after=99, _dbg=False, _ncores=8, _skip=()):
    f = lambda a: np.ascontiguousarray(np.asarray(a, dtype=np.float32))
    x = f(x)
    tab_m, tab_kr, tab_d, tab_g = _tables()
    gains = np.concatenate([_col128(f(e_attn_norm)[0]), _col128(f(ffn_norm)[0]), _col128(f(o_attn_norm)[0]),
                            _col128(f(ffn_norm)[1]), _col128(f(final_norm))], axis=1)
    p32 = _half_perm(32); p16 = _half_perm(16); p64 = _half_perm(64)
    w_in = f(e_w_in)[0]
    kr_cols = 320 + p32
    d_perm = np.arange(512)
    for g in range(8):
        d_perm[g * 64:g * 64 + 16] = g * 64 + p16
    w_in_p = np.concatenate([w_in[:, kr_cols], w_in[:, 352 + d_perm], w_in[:, 864 + d_perm]], axis=1)
    uq = f(e_w_uq)[0]
    uq_perm = np.arange(768)
    for h in range(8):
        uq_perm[h * 96 + 64:h * 96 + 96] = h * 96 + 64 + p32
    uq_p = uq[:, uq_perm]
    g_perm = np.concatenate([p64, 64 + p64])
    qkv = f(o_w_qkv)[0]
    qk_perm = np.arange(1280)
    for h in range(10):
        qk_perm[h * 128:(h + 1) * 128] = h * 128 + g_perm
    qkv_p = qkv[:, qk_perm]
    gq = f(o_q_norm)[0]; gk = f(o_k_norm)[0]
    qn = f(e_q_norm)[0]
    svec = np.zeros((128, 8), np.float32)
    svec[:, 0] = qn[:128]
    svec[:64, 1] = qn[128:192]
    svec[:, 2] = f(e_kv_norm)[0]
    svec[:, 3] = f(e_subln)[0]
    svec[:, 4] = gq; svec[:, 5] = gq[g_perm]; svec[:, 6] = gk; svec[:, 7] = gk[g_perm]
    lamv = np.stack([f(e_lambda_q1)[0], f(e_lambda_k1)[0], f(e_lambda_q2)[0], f(e_lambda_k2)[0]])[None].repeat(128, 0)
    common = {
        "gains": f(gains), "svec": svec, "lamv": f(lamv), "sel": _sel(), "pms": _pms(d_perm[:128], g_perm),
        "e_w_in": w_in, "e_w_in_p": f(w_in_p), "e_w_uq": uq, "e_w_uq_p": f(uq_p), "e_w_ukv": f(e_w_ukv)[0],
        "e_w_out": f(e_w_out)[0], "o_w_qkv": qkv, "o_w_qkv_p": f(qkv_p), "o_w_out": f(o_w_out)[0],
        "w_gate": f(w_gate), "w_up": f(w_up), "w_down": f(w_down),
        "tab_m": tab_m, "tab_kr": tab_kr, "tab_d": tab_d, "tab_g": tab_g,
    }
    key = (_stop_after, _dbg, tuple(_skip))
    if key not in _CACHE:
        _CACHE[key] = build_program(_stop_after, _dbg, tuple(_skip))
    nc = _CACHE[key]
    in_maps = []
    for b in range(_ncores):
        m = dict(common)
        m["xT"] = np.ascontiguousarray(x[b].T)
        in_maps.append(m)
    res = run_bass_kernel_spmd(nc, in_maps, core_ids=list(range(_ncores)))
    if _dbg:
        return res
    out = np.stack([np.asarray(r["outT"]).T for r in res.results], axis=0)
    return np.ascontiguousarray(out.astype(np.float32))
```
